# Optimizing a Trainium2 kernel written in Bass

```python
import math
import jax, jax.numpy as jnp
from jax import lax
import numpy as np

D_MODEL = 1024
BATCH = 8
SEQ = 4096
DEPTH = 4
DEC_BATCH = 2
DEC_SEQ = 8192
PAST_LEN = 128

N_META = 16
N_MIXERS = 4
DN_ALPHA = (2 * DEPTH) ** 0.25
DN_BETA = (8 * DEPTH) ** -0.25
LN_EPS = 1e-5
RMS_EPS = 1e-6
ROPE_THETA = 10000.0

HG_HEADS = 8
HG_FDIM = 128
HG_IDIM = D_MODEL // HG_HEADS
HG_CHUNK = 16
POOL_WINDOWS = (2, 4, 8, 16)
POOL_GROUP = D_MODEL // len(POOL_WINDOWS)
SW_HEADS = 8
SW_KV_HEADS = 2
SW_GROUPS = SW_HEADS // SW_KV_HEADS
SW_HEAD_DIM = D_MODEL // SW_HEADS
SW_WINDOW = 128
SW_BLOCK = 128
MLA_HEADS = 16
MLA_Q_RANK = 256
MLA_KV_RANK = 256
MLA_NOPE = 128
MLA_ROPE = 64
MLA_V = 128
MLA_BLOCK = 128
D_FF = ((8 * D_MODEL + 3 * 256 - 1) // (3 * 256)) * 256

N_A = (DEPTH + 3) // 4
N_B = (DEPTH + 2) // 4
N_C = (DEPTH + 1) // 4
N_D = DEPTH // 4

kernel_name = "hybrid_bidir_hgrn2_pool_swa_mla_encoder"


def layer_norm(x, g, b):
    xf = x.astype(jnp.float32)
    mu = jnp.mean(xf, axis=-1, keepdims=True)
    xc = xf - mu
    var = jnp.mean(xc * xc, axis=-1, keepdims=True)
    return (xc * lax.rsqrt(var + LN_EPS) * g.astype(jnp.float32) + b.astype(jnp.float32)).astype(x.dtype)


def rms_norm(x, g):
    xf = x.astype(jnp.float32)
    return (xf * lax.rsqrt(jnp.mean(xf * xf, axis=-1, keepdims=True) + RMS_EPS) * g.astype(jnp.float32)).astype(x.dtype)


def rope(x, pos):
    d = x.shape[-1]
    inv = 1.0 / (ROPE_THETA ** (jnp.arange(0, d, 2, dtype=jnp.float32) / d))
    ang = pos.astype(jnp.float32)[:, None] * inv[None, :]
    cos = jnp.cos(ang)[:, None, :]
    sin = jnp.sin(ang)[:, None, :]
    xf = x.astype(jnp.float32)
    x1, x2 = xf[..., : d // 2], xf[..., d // 2:]
    return jnp.concatenate([x1 * cos - x2 * sin, x2 * cos + x1 * sin], axis=-1).astype(x.dtype)


def pad_time(a, front, back):
    return jnp.pad(a, [(0, 0), (front, back)] + [(0, 0)] * (a.ndim - 2))


def hgrn2_scan(q, k, v, logf):
    B, T, H, F = q.shape
    E = v.shape[-1]
    n = T // HG_CHUNK

    def chunks(a):
        return jnp.moveaxis(a.reshape(B, n, HG_CHUNK, H, a.shape[-1]), 1, 0)

    causal = jnp.tril(jnp.ones((HG_CHUNK, HG_CHUNK), dtype=bool))

    def step(S, inp):
        qc, kc, vc, lf = inp
        b = jnp.cumsum(lf, axis=1)
        bl = b[:, -1:]
        qe = qc * jnp.exp(b)
        ke = kc * jnp.exp(-b)
        A = jnp.where(causal, jnp.einsum('bthf,bshf->bhts', qe, ke), 0.0)
        o = jnp.einsum('bhts,bshe->bthe', A, vc) + jnp.einsum('bthf,bhfe->bthe', qe, S)
        S = jnp.exp(bl[:, 0])[..., None] * S + jnp.einsum('bshf,bshe->bhfe', kc * jnp.exp(bl - b), vc)
        return S, o

    S0 = jnp.zeros((B, H, F, E), jnp.float32)
    _, o = lax.scan(step, S0, (chunks(q), chunks(k), chunks(v), chunks(logf)))
    return jnp.moveaxis(o, 0, 1).reshape(B, T, H, E)


def hgrn2_mixer(x, w_in, w_out, norm_g, lb):
    B, T, _ = x.shape
    HF = HG_HEADS * HG_FDIM
    HE = HG_HEADS * HG_IDIM
    proj = x @ w_in
    q, i, ff, fb, g = jnp.split(proj, [HF, HF + HE, 2 * HF + HE, 3 * HF + HE], axis=-1)
    heads = lambda a: a.reshape(B, T, HG_HEADS, -1).astype(jnp.float32)
    q = jax.nn.silu(heads(q))
    v = heads(i)
    lbh = lb.astype(jnp.float32).reshape(HG_HEADS, HG_FDIM)

    def gates(a):
        a = heads(a)
        f = lbh + (1.0 - lbh) * jax.nn.sigmoid(a)
        return (1.0 - lbh) * jax.nn.sigmoid(-a), jnp.log(f)

    k_f, lf_f = gates(ff)
    k_b, lf_b = gates(fb)
    o_fwd = hgrn2_scan(q, k_f, v, lf_f)
    flip = lambda a: jnp.flip(a, axis=1)
    o_bwd = flip(hgrn2_scan(flip(q), flip(k_b), flip(v), flip(lf_b)))
    o = rms_norm(o_fwd + o_bwd, norm_g) * jax.nn.silu(heads(g))
    return o.reshape(B, T, HE).astype(x.dtype) @ w_out


def pool_mixer(x, w_grp, scale):
    B, T, D = x.shape
    xf = x.astype(jnp.float32)
    cs = jnp.concatenate([jnp.zeros((B, 1, D), jnp.float32), jnp.cumsum(xf, axis=1)], axis=1)
    t = jnp.arange(T)
    outs = []
    for gi, w in enumerate(POOL_WINDOWS):
        sl = slice(gi * POOL_GROUP, (gi + 1) * POOL_GROUP)
        lo = jnp.clip(t - w // 2, 0, T)
        hi = jnp.clip(t + w // 2, 0, T)
        csg = cs[:, :, sl]
        mean = (csg[:, hi] - csg[:, lo]) / (hi - lo).astype(jnp.float32)[:, None]
        outs.append(mean - xf[:, :, sl])
    p = jnp.stack(outs, axis=2).astype(x.dtype)
    y = jnp.einsum('btgc,gcd->btgd', p, w_grp).reshape(B, T, D)
    return y * scale


def swa_mixer(x, w_qkv, w_out, sink):
    B, T, _ = x.shape
    HQ, HK, G, HD, BLK = SW_HEADS, SW_KV_HEADS, SW_GROUPS, SW_HEAD_DIM, SW_BLOCK
    pos = jnp.arange(T)
    q, k, v = jnp.split(x @ w_qkv, [HQ * HD, (HQ + HK) * HD], axis=-1)
    q = rope(q.reshape(B, T, HQ, HD), pos).reshape(B, T, HK, G, HD)
    k = rope(k.reshape(B, T, HK, HD), pos)
    v = v.reshape(B, T, HK, HD)
    pad = (-T) % BLK
    Tp = T + pad
    nb = Tp // BLK
    qb = pad_time(q, pad, 0).reshape(B, nb, BLK, HK, G, HD)
    kb = pad_time(k, pad + BLK, BLK).reshape(B, nb + 2, BLK, HK, HD)
    vb = pad_time(v, pad + BLK, BLK).reshape(B, nb + 2, BLK, HK, HD)
    kw = jnp.concatenate([kb[:, :-2], kb[:, 1:-1], kb[:, 2:]], axis=2)
    vw = jnp.concatenate([vb[:, :-2], vb[:, 1:-1], vb[:, 2:]], axis=2)
    qpos = jnp.arange(Tp).reshape(nb, BLK) - pad
    kpos = jnp.arange(nb)[:, None] * BLK + jnp.arange(3 * BLK)[None, :] - BLK - pad
    valid = ((kpos[:, None, :] >= 0) & (kpos[:, None, :] < T)
             & (jnp.abs(qpos[:, :, None] - kpos[:, None, :]) <= SW_WINDOW))
    s = jnp.einsum('bnqkgd,bnskd->bnkgqs', qb, kw).astype(jnp.float32) * (HD ** -0.5)
    s = jnp.where(valid[None, :, None, None], s, -jnp.inf)
    sk = sink.astype(jnp.float32).reshape(HK, G)[None, None, :, :, None, None]
    m = jnp.maximum(jnp.max(s, axis=-1, keepdims=True), sk)
    p = jnp.exp(s - m)
    den = jnp.sum(p, axis=-1, keepdims=True) + jnp.exp(sk - m)
    o = jnp.einsum('bnkgqs,bnskd->bnqkgd', (p / den).astype(x.dtype), vw)
    o = o.reshape(B, Tp, HQ * HD)[:, pad:]
    return o @ w_out


def mla_mixer(x, w_dq, q_norm_g, w_uq, w_dkv, kv_norm_g, w_ukv, w_out):
    B, T, _ = x.shape
    H, BLK = MLA_HEADS, MLA_BLOCK
    pos = jnp.arange(T)
    cq = rms_norm(x @ w_dq, q_norm_g)
    q = (cq @ w_uq).reshape(B, T, H, MLA_NOPE + MLA_ROPE)
    q_nope, q_rope = q[..., :MLA_NOPE], rope(q[..., MLA_NOPE:], pos)
    ckv = x @ w_dkv
    c = rms_norm(ckv[..., :MLA_KV_RANK], kv_norm_g)
    k_rope = rope(ckv[..., MLA_KV_RANK:][:, :, None, :], pos)[:, :, 0]
    kv = (c @ w_ukv).reshape(B, T, H, MLA_NOPE + MLA_V)
    k_nope, v = kv[..., :MLA_NOPE], kv[..., MLA_NOPE:]
    pad = (-T) % BLK
    nb = (T + pad) // BLK
    to_blocks = lambda a: jnp.moveaxis(pad_time(a, pad, 0).reshape(B, nb, BLK, H, a.shape[-1]), 1, 0)
    scale = (MLA_NOPE + MLA_ROPE) ** -0.5

    def block(args):
        qn, qr = args
        s = (jnp.einsum('bqhd,bkhd->bhqk', qn, k_nope)
             + jnp.einsum('bqhd,bkd->bhqk', qr, k_rope)).astype(jnp.float32) * scale
        p = jax.nn.softmax(s, axis=-1)
        return jnp.einsum('bhqk,bkhd->bqhd', p.astype(x.dtype), v)

    o = lax.map(block, (to_blocks(q_nope), to_blocks(q_rope)))
    o = jnp.moveaxis(o, 0, 1).reshape(B, nb * BLK, H * MLA_V)[:, pad:]
    return o @ w_out


def swiglu(x, w_gu, w_down):
    g, u = jnp.split(x @ w_gu, 2, axis=-1)
    return (jax.nn.silu(g) * u) @ w_down


def trunk(x, p):
    B = x.shape[0]
    meta = jnp.broadcast_to(p['meta_tokens'].astype(x.dtype)[None], (B, N_META, D_MODEL))
    h = jnp.concatenate([meta, x], axis=1)
    lb_all = jnp.cumsum(jax.nn.softmax(p['hg_lb_logits'].astype(jnp.float32), axis=0), axis=0)
    for i in range(DEPTH):
        kind, j = i % N_MIXERS, i // N_MIXERS
        if kind == 0:
            y = hgrn2_mixer(h, p['a_w_in'][j], p['a_w_out'][j], p['a_norm_g'][j], lb_all[i])
        elif kind == 1:
            y = pool_mixer(h, p['b_w_grp'][j], p['b_scale'][j])
        elif kind == 2:
            y = swa_mixer(h, p['c_w_qkv'][j], p['c_w_out'][j], p['c_sink'][j])
        else:
            y = mla_mixer(h, p['d_w_dq'][j], p['d_q_norm_g'][j], p['d_w_uq'][j], p['d_w_dkv'][j],
                          p['d_kv_norm_g'][j], p['d_w_ukv'][j], p['d_w_out'][j])
        h = layer_norm(DN_ALPHA * h + y, p['ln_g'][i, 0], p['ln_b'][i, 0])
        h = layer_norm(DN_ALPHA * h + swiglu(h, p['ffn_w_gu'][i], p['ffn_w_down'][i]), p['ln_g'][i, 1], p['ln_b'][i, 1])
    return h[:, N_META:]


def setup_inputs(seed: int = 0) -> dict:
    key = jax.random.key(seed)
    ks = jax.random.split(key, 24)
    f32 = jnp.float32
    nrm = lambda k, shape, s: jax.random.normal(k, shape, f32) * s
    HF = HG_HEADS * HG_FDIM
    HE = HG_HEADS * HG_IDIM
    return {
        'x_prompt': nrm(ks[0], (BATCH, SEQ, D_MODEL), 1.0),
        'x_sample': nrm(ks[1], (DEC_BATCH, DEC_SEQ, D_MODEL), 1.0),
        'meta_tokens': nrm(ks[2], (N_META, D_MODEL), 1.0),
        'hg_lb_logits': nrm(ks[3], (DEPTH + 1, HF), 0.1),
        'a_w_in': nrm(ks[4], (N_A, D_MODEL, 3 * HF + 2 * HE), D_MODEL ** -0.5),
        'a_w_out': nrm(ks[5], (N_A, HE, D_MODEL), DN_BETA * HE ** -0.5),
        'a_norm_g': 1.0 + nrm(ks[6], (N_A, HG_IDIM), 0.02),
        'b_w_grp': nrm(ks[7], (N_B, len(POOL_WINDOWS), POOL_GROUP, POOL_GROUP), DN_BETA * POOL_GROUP ** -0.5),
        'b_scale': 1.0 + nrm(ks[8], (N_B, D_MODEL), 0.02),
        'c_w_qkv': nrm(ks[9], (N_C, D_MODEL, (SW_HEADS + 2 * SW_KV_HEADS) * SW_HEAD_DIM), D_MODEL ** -0.5),
        'c_w_out': nrm(ks[10], (N_C, SW_HEADS * SW_HEAD_DIM, D_MODEL), DN_BETA * (SW_HEADS * SW_HEAD_DIM) ** -0.5),
        'c_sink': nrm(ks[11], (N_C, SW_HEADS), 1.0),
        'd_w_dq': nrm(ks[12], (N_D, D_MODEL, MLA_Q_RANK), D_MODEL ** -0.5),
        'd_q_norm_g': 1.0 + nrm(ks[13], (N_D, MLA_Q_RANK), 0.02),
        'd_w_uq': nrm(ks[14], (N_D, MLA_Q_RANK, MLA_HEADS * (MLA_NOPE + MLA_ROPE)), MLA_Q_RANK ** -0.5),
        'd_w_dkv': nrm(ks[15], (N_D, D_MODEL, MLA_KV_RANK + MLA_ROPE), D_MODEL ** -0.5),
        'd_kv_norm_g': 1.0 + nrm(ks[16], (N_D, MLA_KV_RANK), 0.02),
        'd_w_ukv': nrm(ks[17], (N_D, MLA_KV_RANK, MLA_HEADS * (MLA_NOPE + MLA_V)), MLA_KV_RANK ** -0.5),
        'd_w_out': nrm(ks[18], (N_D, MLA_HEADS * MLA_V, D_MODEL), DN_BETA * (MLA_HEADS * MLA_V) ** -0.5),
        'ffn_w_gu': nrm(ks[19], (DEPTH, D_MODEL, 2 * D_FF), D_MODEL ** -0.5),
        'ffn_w_down': nrm(ks[20], (DEPTH, D_FF, D_MODEL), DN_BETA * D_FF ** -0.5),
        'ln_g': 1.0 + nrm(ks[21], (DEPTH, 2, D_MODEL), 0.02),
        'ln_b': nrm(ks[22], (DEPTH, 2, D_MODEL), 0.02),
    }


def reference(x_prompt, x_sample, meta_tokens, hg_lb_logits, a_w_in, a_w_out, a_norm_g, b_w_grp, b_scale,
              c_w_qkv, c_w_out, c_sink, d_w_dq, d_q_norm_g, d_w_uq, d_w_dkv, d_kv_norm_g, d_w_ukv, d_w_out,
              ffn_w_gu, ffn_w_down, ln_g, ln_b):
    params = {
        'meta_tokens': meta_tokens, 'hg_lb_logits': hg_lb_logits,
        'a_w_in': a_w_in, 'a_w_out': a_w_out, 'a_norm_g': a_norm_g,
        'b_w_grp': b_w_grp, 'b_scale': b_scale,
        'c_w_qkv': c_w_qkv, 'c_w_out': c_w_out, 'c_sink': c_sink,
        'd_w_dq': d_w_dq, 'd_q_norm_g': d_q_norm_g, 'd_w_uq': d_w_uq, 'd_w_dkv': d_w_dkv,
        'd_kv_norm_g': d_kv_norm_g, 'd_w_ukv': d_w_ukv, 'd_w_out': d_w_out,
        'ffn_w_gu': ffn_w_gu, 'ffn_w_down': ffn_w_down, 'ln_g': ln_g, 'ln_b': ln_b,
    }
    y_prompt = trunk(x_prompt, params)
    y_sample = trunk(x_sample, params)
    return (y_prompt, y_sample)
```

```python
import math
import numpy as np
import concourse.bass as bass
import concourse.mybir as mybir
from concourse.bass_utils import run_bass_kernel_spmd

F32 = mybir.dt.float32
BF16 = mybir.dt.bfloat16
ALU = mybir.AluOpType
AF = mybir.ActivationFunctionType

D = 1024
DFF = 2816
N_META = 16
DEPTH = 4
DN_ALPHA = (2 * DEPTH) ** 0.25
LN_EPS = 1e-5
RMS_EPS = 1e-6
ARENA_WORDS = 52000


class Dep:
    __slots__ = ("w", "r")

    def __init__(self):
        self.w = {}
        self.r = {}


class KB:
    ENG = ("pe", "act", "dve", "pool", "sp")

    def __init__(self, nc):
        self.nc = nc
        self.stream = {e: [] for e in self.ENG}
        self.esem = {e: nc.alloc_semaphore("s_" + e) for e in ("pe", "act", "dve", "pool")}
        self.cnt = {e: 0 for e in self.esem}
        self.known = {e: {} for e in self.ENG}
        self.dsems = []
        self.dsem_idx = 0
        self.dsem_base = 0
        self.arena = nc.alloc_sbuf_tensor("arena", [128, ARENA_WORDS], F32)
        self.psum = nc.alloc_psum_tensor("psum", [128, 4096], F32)
        self.pbank = [Dep() for _ in range(8)]
        self.aoff = 0
        self.n_inst = 0

    def reset_arena(self, keep=0):
        self.aoff = keep

    def alloc(self, words):
        words = (words + 7) // 8 * 8
        o = self.aoff
        self.aoff += words
        assert self.aoff <= ARENA_WORDS, f"SBUF arena overflow {self.aoff}"
        return self.arena[:, o:o + words]

    def f32(self, n):
        return self.alloc(n)

    def bf(self, n):
        return self.alloc((n + 1) // 2).bitcast(BF16)[:, 0:n]

    def bank(self, i, n=512):
        return self.psum[:, i * 512:i * 512 + n]

    def new_dsem(self, name):
        if self.dsem_idx < len(self.dsems):
            s = self.dsems[self.dsem_idx]
        else:
            s = [self.nc.alloc_semaphore(f"d{len(self.dsems)}"), 0]
            self.dsems.append(s)
        self.dsem_idx += 1
        return s

    def sub_begin(self):
        self.barrier()
        self.dsem_idx = self.dsem_base

    def _collect(self, r, w):
        d = {}
        for t in r:
            for num, tok in t.w.items():
                if num not in d or d[num][1] < tok[1]:
                    d[num] = tok
        for t in w:
            for m in (t.w, t.r):
                for num, tok in m.items():
                    if num not in d or d[num][1] < tok[1]:
                        d[num] = tok
        return d

    def _wait(self, eng, deps):
        kn = self.known[eng]
        pe_num = self.esem["pe"].num
        for num, (sem, val) in deps.items():
            if eng == "pe" and num == pe_num:
                continue
            if kn.get(num, 0) < val:
                self.stream[eng].append(("w", sem, val))
                kn[num] = val
                self.n_inst += 1

    def _commit(self, tok, r, w):
        num = tok[0].num
        for t in r:
            t.r[num] = tok
        for t in w:
            t.w = {num: tok}
            t.r = {}

    def op(self, eng, fn, r=(), w=(), sig=True):
        self._wait(eng, self._collect(r, w))
        sem = self.esem[eng]
        if sig:
            self.cnt[eng] += 1
            tok = (sem, self.cnt[eng])
            self.stream[eng].append(("i", fn, sem, 1))
        else:
            tok = (sem, self.cnt[eng] + 1)
            self.stream[eng].append(("i", fn, None, 0))
        self.n_inst += 1
        self._commit(tok, r, w)

    def dma(self, q, out, in_, ds, r=(), w=()):
        self._wait(q, self._collect(r, w))
        ds[1] += 16
        tok = (ds[0], ds[1])
        self.stream[q].append(("i", lambda e: e.dma_start(out=out, in_=in_), ds[0], 16))
        self.n_inst += 1
        self._commit(tok, r, w)

    def barrier(self):
        deps = {}
        for e, sem in self.esem.items():
            if self.cnt[e] > 0:
                deps[sem.num] = (sem, self.cnt[e])
        for s in self.dsems:
            if s[1] > 0:
                deps[s[0].num] = (s[0], s[1])
        for e in self.ENG:
            self._wait(e, deps)

    def mm(self, out, lhsT, rhs, start, stop, r=(), w=(), sig=False):
        self.op("pe", lambda e: e.matmul(out, lhsT, rhs, start=start, stop=stop), r=r, w=w, sig=sig)

    def tr(self, out, in_, ident, r=(), w=(), sig=False):
        self.op("pe", lambda e: e.transpose(out, in_, ident), r=r, w=w, sig=sig)


    def act(self, out, in_, func, bias=0.0, scale=1.0, accum_out=None, r=(), w=()):
        if accum_out is None:
            self.op("act", lambda e: e.activation(out=out, in_=in_, func=func, bias=bias, scale=scale), r=r, w=w)
        else:
            self.op("act", lambda e: e.activation(out=out, in_=in_, func=func, bias=bias, scale=scale,
                                                  accum_out=accum_out), r=r, w=w)

    def copy(self, eng, out, in_, r=(), w=()):
        if eng == "act":
            self.op("act", lambda e: e.copy(out=out, in_=in_), r=r, w=w)
        else:
            self.op(eng, lambda e: e.tensor_copy(out=out, in_=in_), r=r, w=w)

    def tt(self, eng, out, in0, in1, op, r=(), w=()):
        self.op(eng, lambda e: e.tensor_tensor(out=out, in0=in0, in1=in1, op=op), r=r, w=w)

    def ts(self, eng, out, in0, s1, s2, op0, op1=None, r=(), w=()):
        if op1 is None:
            self.op(eng, lambda e: e.tensor_scalar(out=out, in0=in0, scalar1=s1, scalar2=None, op0=op0), r=r, w=w)
        else:
            self.op(eng, lambda e: e.tensor_scalar(out=out, in0=in0, scalar1=s1, scalar2=s2, op0=op0, op1=op1), r=r, w=w)

    def stt(self, eng, out, in0, scalar, in1, op0, op1, r=(), w=()):
        self.op(eng, lambda e: e.scalar_tensor_tensor(out=out, in0=in0, scalar=scalar, in1=in1, op0=op0, op1=op1),
                r=r, w=w)

    def bn_stats(self, out, in_, r=(), w=()):
        self.op("dve", lambda e: e.bn_stats(out=out, in_=in_), r=r, w=w)

    def bn_aggr(self, out, in_, r=(), w=()):
        self.op("dve", lambda e: e.bn_aggr(out=out, in_=in_), r=r, w=w)

    def recip(self, out, in_, r=(), w=()):
        self.op("dve", lambda e: e.reciprocal(out=out, in_=in_), r=r, w=w)

    def memset(self, eng, out, val, r=(), w=()):
        self.op(eng, lambda e: e.memset(out, val), r=r, w=w)

    def emit(self):
        nc = self.nc
        streams = self.stream
        with nc.Block() as block:
            def run(e, lst):
                for ent in lst:
                    if ent[0] == "w":
                        e.wait_ge(ent[1], ent[2])
                    else:
                        ins = ent[1](e)
                        if ent[2] is not None:
                            ins.then_inc(ent[2], ent[3])

            @block.tensor
            def _(e):
                run(e, streams["pe"])

            @block.scalar
            def _(e):
                run(e, streams["act"])

            @block.vector
            def _(e):
                run(e, streams["dve"])

            @block.gpsimd
            def _(e):
                run(e, streams["pool"])

            @block.sync
            def _(e):
                run(e, streams["sp"])


class Seq:
    def __init__(self, nc, name, T):
        self.T = T
        self.name = name
        self.ngroups = (T + 511) // 512
        self.H = [nc.dram_tensor(f"{name}_H{i}", [T, D], F32, kind="Internal").ap() for i in range(2)]
        self.HT = [nc.dram_tensor(f"{name}_HT{i}", [D, T], BF16, kind="Internal").ap() for i in range(2)]
        self.dH = [[Dep() for _ in range(self.ngroups)] for _ in range(2)]
        self.dHT = [[Dep() for _ in range(self.ngroups)] for _ in range(2)]
        self.cur = 0

    def group(self, g):
        t0 = g * 512
        n = min(512, self.T - t0)
        return t0, n


def load_weight_bf16(k, dst3, src2, ds, wdep, kchunks):
    for c in range(kchunks):
        k.dma("pool", dst3[:, c, :], src2[c * 128:(c + 1) * 128, :], ds, w=[wdep])


def ln_epilogue(k, C, seq, dst, g, t0, nv, ti, y_ps, y_deps, hbuf, hdep, gam, bet, cdep, st, hT_out, hTo_dep,
                tr_bank):
    P = slice(0, nv)
    stats, mv, rstd, hb, sdep, hbdep, ds_out = st
    k.stt("dve", hbuf[P, :], hbuf[P, :], float(DN_ALPHA), y_ps[P, :], ALU.mult, ALU.add, r=list(y_deps), w=[hdep])
    ln_rows(k, C, seq, dst, g, t0, nv, ti, hbuf, hdep, gam, bet, cdep, st, hT_out, hTo_dep, tr_bank)


def ln_rows(k, C, seq, dst, g, t0, nv, ti, hbuf, hdep, gam, bet, cdep, st, hT_out, hTo_dep, tr_bank):
    P = slice(0, nv)
    stats, mv, rstd, hb, sdep, hbdep, ds_out = st
    for j in range(2):
        k.bn_stats(stats[P, 6 * j:6 * j + 6], hbuf[P, 512 * j:512 * j + 512], r=[hdep], w=[sdep])
    k.bn_aggr(mv[P, 0:2], stats[P, 0:12], r=[sdep], w=[sdep])
    k.ts("dve", rstd[P, 0:1], mv[P, 1:2], float(LN_EPS), None, ALU.add, r=[sdep], w=[sdep])
    k.tt("pool", rstd[P, 0:1], rstd[P, 0:1], C["neghalf"][P, 0:1], ALU.pow, r=[sdep, C["dep2"]], w=[sdep])
    k.ts("dve", hbuf[P, :], hbuf[P, :], mv[P, 0:1], rstd[P, 0:1], ALU.subtract, ALU.mult, r=[sdep], w=[hdep])
    k.tt("pool", hbuf[P, :], hbuf[P, :], gam[P, :], ALU.mult, r=[cdep], w=[hdep])
    k.tt("pool", hbuf[P, :], hbuf[P, :], bet[P, :], ALU.add, r=[cdep], w=[hdep])
    k.dma("sp", seq.H[dst][t0:t0 + nv, :], hbuf[P, :], ds_out, r=[hdep], w=[seq.dH[dst][g]])
    k.copy("act", hb[P, :], hbuf[P, :], r=[hdep], w=[hbdep])
    transpose_rows(k, C, hb, hbdep, nv, hT_out, hTo_dep, ti * 128, tr_bank)


def transpose_rows(k, C, hb, hbdep, nv, hT_out, hTo_dep, col0, tr_bank):
    pt = k.bank(tr_bank)
    for half in range(2):
        for j in range(4):
            c = half * 4 + j
            k.mm(pt[:, j * 128:j * 128 + nv], hb[0:nv, c * 128:(c + 1) * 128], C["ident"][0:nv, 0:nv], True, True,
                 r=[hbdep, C["dep"]], w=[k.pbank[tr_bank]], sig=(j == 3))
        k.copy("act", hT_out[:, half * 4:half * 4 + 4, col0:col0 + nv],
               pt.rearrange("p (c t) -> p c t", c=4)[:, :, 0:nv], r=[k.pbank[tr_bank]], w=[hTo_dep])


def ffn_sublayer(k, C, seqs, W, li):
    nc = k.nc
    k.sub_begin()
    k.reset_arena(C["keep"])
    wgu = k.bf(8 * 2 * DFF).rearrange("p (c n) -> p c n", c=8)
    wd = k.bf(22 * D).rearrange("p (c n) -> p c n", c=22)
    gam = k.f32(D)
    bet = k.f32(D)
    wdep, cdep = Dep(), Dep()
    dsw = k.new_dsem(f"ffw{li}")
    load_weight_bf16(k, wgu, W["ffn_w_gu"][li], dsw, wdep, 8)
    load_weight_bf16(k, wd, W["ffn_w_down"][li], dsw, wdep, 22)
    k.dma("sp", gam, W["ln_g"][li, 1:2, :].partition_broadcast(128), dsw, w=[cdep])
    k.dma("sp", bet, W["ln_b"][li, 1:2, :].partition_broadcast(128), dsw, w=[cdep])
    hT_in = k.bf(8 * 512).rearrange("p (c n) -> p c n", c=8)
    aT = k.bf(22 * 512).rearrange("p (c n) -> p c n", c=22)
    sil = [k.f32(512) for _ in range(2)]
    hbufs = [k.f32(D) for _ in range(3)]
    hT_out = k.bf(8 * 512).rearrange("p (c n) -> p c n", c=8)
    stats = k.f32(16)
    mv = k.f32(8)
    rstd = k.f32(8)
    hb = k.bf(D)
    d_hTin, d_aT, d_hTo, d_s, d_hb = Dep(), Dep(), Dep(), Dep(), Dep()
    d_sil = [Dep(), Dep()]
    d_hbuf = [Dep() for _ in range(3)]
    ds_in = k.new_dsem(f"ffi{li}")
    ds_h = [k.new_dsem(f"ffh{li}_{i}") for i in range(3)]
    ds_o = k.new_dsem(f"ffo{li}")
    gu_banks = [0, 1, 2]
    y_banks = [(3, 4), (5, 6)]
    tr_bank = 7
    gi = 0
    yi = 0
    hi = 0
    for seq in seqs:
        src = seq.cur
        dst = 1 - src
        HTv = seq.HT[src].rearrange("(c p) t -> p c t", p=128)
        HTo = seq.HT[dst].rearrange("(c p) t -> p c t", p=128)
        for g in range(seq.ngroups):
            t0, n = seq.group(g)
            k.dma("sp", hT_in[:, :, 0:n], HTv[:, :, t0:t0 + n], ds_in, r=[seq.dHT[src][g]], w=[d_hTin])
            for c in range(22):
                bg = gu_banks[gi % 3]
                bu = gu_banks[(gi + 1) % 3]
                gi += 2
                for (b, col0) in ((bg, c * 128), (bu, DFF + c * 128)):
                    for kc in range(8):
                        k.mm(k.bank(b, n), wgu[:, kc, col0:col0 + 128], hT_in[:, kc, 0:n], kc == 0, kc == 7,
                             r=[wdep, d_hTin], w=[k.pbank[b]], sig=(kc == 7))
                s = c % 2
                k.act(sil[s][:, 0:n], k.bank(bg, n), AF.Silu, r=[k.pbank[bg]], w=[d_sil[s]])
                k.tt("dve", aT[:, c, 0:n], sil[s][:, 0:n], k.bank(bu, n), ALU.mult, r=[k.pbank[bu], d_sil[s]], w=[d_aT])
            ntile = (n + 127) // 128
            for ti in range(ntile):
                nv = min(128, n - ti * 128)
                tt0 = t0 + ti * 128
                hbuf = hbufs[hi % 3]
                hdep = d_hbuf[hi % 3]
                dsh = ds_h[hi % 3]
                hi += 1
                k.dma("sp", hbuf[0:nv, :], seq.H[src][tt0:tt0 + nv, :], dsh, r=[seq.dH[src][g]], w=[hdep])
                yb = y_banks[yi % 2]
                yi += 1
                y_ps = k.psum[:, yb[0] * 512:yb[0] * 512 + 1024]
                for half in range(2):
                    for c in range(22):
                        k.mm(k.bank(yb[half])[0:nv, :], aT[:, c, ti * 128:ti * 128 + nv], wd[:, c, half * 512:(half + 1) * 512],
                             c == 0, c == 21, r=[wdep, d_aT], w=[k.pbank[yb[half]]], sig=(c == 21))
                ln_epilogue(k, C, seq, dst, g, tt0, nv, ti, y_ps, [k.pbank[yb[0]], k.pbank[yb[1]]], hbuf, hdep, gam, bet,
                            cdep, (stats, mv, rstd, hb, d_s, d_hb, dsh), hT_out, d_hTo, tr_bank)
            k.dma("sp", HTo[:, :, t0:t0 + n], hT_out[:, :, 0:n], ds_o, r=[d_hTo], w=[seq.dHT[dst][g]])
        seq.cur = dst


def alloc_ln_state(k, li, which, W, tag):
    gam = k.f32(D)
    bet = k.f32(D)
    cdep = Dep()
    dsw = k.new_dsem(f"{tag}c{li}")
    k.dma("sp", gam, W["ln_g"][li, which:which + 1, :].partition_broadcast(128), dsw, w=[cdep])
    k.dma("sp", bet, W["ln_b"][li, which:which + 1, :].partition_broadcast(128), dsw, w=[cdep])
    stats = k.f32(16)
    mv = k.f32(8)
    rstd = k.f32(8)
    hb = k.bf(D)
    hT_out = k.bf(8 * 512).rearrange("p (c n) -> p c n", c=8)
    return dict(gam=gam, bet=bet, cdep=cdep, stats=stats, mv=mv, rstd=rstd, hb=hb, hT_out=hT_out, d_s=Dep(), d_hb=Dep(),
                d_hTo=Dep(), ds_o=k.new_dsem(f"{tag}o{li}"))


def pool_sublayer(k, C, seqs, W, li):
    k.sub_begin()
    k.reset_arena(C["keep"])
    L = alloc_ln_state(k, li, 0, W, "pl")
    wg = k.bf(4 * 2 * 256).rearrange("p (g c n) -> p g c n", g=4, c=2)
    wdep = Dep()
    dsw = k.new_dsem("plw")
    for g4 in range(4):
        for c in range(2):
            k.dma("pool", wg[:, g4, c, :], W["b_w_grp"][0, g4, c * 128:(c + 1) * 128, :], dsw, w=[wdep])
    scl = k.f32(D)
    k.dma("sp", scl, W["b_scale"][0:1, :].partition_broadcast(128), dsw, w=[wdep])
    PADW = 512 + 16
    xin = k.bf(8 * PADW).rearrange("p (c n) -> p c n", c=8)
    xf = k.f32(8 * PADW).rearrange("p (c n) -> p c n", c=8)
    sa = k.f32(2 * PADW).rearrange("p (c n) -> p c n", c=2)
    sb = k.f32(2 * PADW).rearrange("p (c n) -> p c n", c=2)
    icn = k.f32(4 * 512).rearrange("p (c n) -> p c n", c=4)
    pT = k.bf(8 * 512).rearrange("p (c n) -> p c n", c=8)
    ysc = k.f32(D)
    hbufs = [k.f32(D) for _ in range(2)]
    d_x, d_xf, d_sa, d_sb, d_ic, d_pT, d_y = Dep(), Dep(), Dep(), Dep(), Dep(), Dep(), Dep()
    d_hbuf = [Dep(), Dep()]
    ds_in = k.new_dsem("pli")
    ds_ic = k.new_dsem("plic")
    ds_h = [k.new_dsem(f"plh{i}") for i in range(2)]
    hi = 0
    yi = 0
    y_banks = [(0, 1), (2, 3)]
    for si, seq in enumerate(seqs):
        src = seq.cur
        dst = 1 - src
        T = seq.T
        HTv = seq.HT[src].rearrange("(c p) t -> p c t", p=128)
        HTo = seq.HT[dst].rearrange("(c p) t -> p c t", p=128)
        for g in range(seq.ngroups):
            t0, n = seq.group(g)
            lo = max(0, t0 - 8)
            hi_t = min(T, t0 + n + 8)
            off = lo - (t0 - 8)
            gdeps = [seq.dHT[src][gg] for gg in range(max(0, g - 1), min(seq.ngroups, g + 2))]
            k.memset("pool", xin[:, :, :], 0.0, w=[d_x])
            k.dma("sp", xin[:, :, off:off + (hi_t - lo)], HTv[:, :, lo:hi_t], ds_in, r=gdeps, w=[d_x])
            k.copy("dve", xf[:, :, :], xin[:, :, :], r=[d_x], w=[d_xf])
            k.dma("sp", icn[:, :, 0:n], C["invcnt"][si][:, t0:t0 + n].partition_broadcast(128), ds_ic, w=[d_ic])
            for g4 in range(4):
                cs = slice(2 * g4, 2 * g4 + 2)
                w = 2 << g4
                k.tt("dve", sa[:, :, 1:PADW], xf[:, cs, 0:PADW - 1], xf[:, cs, 1:PADW], ALU.add, r=[d_xf], w=[d_sa])
                cur, curd, oth, othd = sa, d_sa, sb, d_sb
                lo_v = 1
                hi_v = PADW
                sh = 1
                while (2 * sh) < w:
                    a0 = lo_v + sh
                    a1 = hi_v - sh
                    k.tt("dve", oth[:, :, a0:a1], cur[:, :, a0 - sh:a1 - sh], cur[:, :, a0 + sh:a1 + sh], ALU.add,
                         r=[curd], w=[othd])
                    cur, curd, oth, othd = oth, othd, cur, curd
                    lo_v, hi_v = a0, a1
                    sh *= 2
                assert lo_v <= 8 and hi_v >= 8 + n
                k.tt("dve", cur[:, :, 8:8 + n], cur[:, :, 8:8 + n],
                     icn[:, g4:g4 + 1, 0:n].to_broadcast([128, 2, n]), ALU.mult, r=[d_ic], w=[curd])
                k.tt("dve", pT[:, cs, 0:n], cur[:, :, 8:8 + n], xf[:, cs, 8:8 + n], ALU.subtract, r=[curd, d_xf], w=[d_pT])
            ntile = (n + 127) // 128
            for ti in range(ntile):
                nv = min(128, n - ti * 128)
                tt0 = t0 + ti * 128
                hbuf = hbufs[hi % 2]
                hdep = d_hbuf[hi % 2]
                dsh = ds_h[hi % 2]
                hi += 1
                k.dma("sp", hbuf[0:nv, :], seq.H[src][tt0:tt0 + nv, :], dsh, r=[seq.dH[src][g]], w=[hdep])
                yb = y_banks[yi % 2]
                yi += 1
                y_ps = k.psum[:, yb[0] * 512:yb[0] * 512 + 1024]
                for g4 in range(4):
                    b = yb[g4 // 2]
                    for c in range(2):
                        k.mm(k.bank(b)[0:nv, (g4 % 2) * 256:(g4 % 2) * 256 + 256], pT[:, 2 * g4 + c, ti * 128:ti * 128 + nv],
                             wg[:, g4, c, :], c == 0, c == 1, r=[wdep, d_pT], w=[k.pbank[b]], sig=(c == 1))
                k.tt("dve", ysc[0:nv, :], y_ps[0:nv, :], scl[0:nv, :], ALU.mult, r=[k.pbank[yb[0]], k.pbank[yb[1]], wdep],
                     w=[d_y])
                k.stt("dve", hbuf[0:nv, :], hbuf[0:nv, :], float(DN_ALPHA), ysc[0:nv, :], ALU.mult, ALU.add, r=[d_y],
                      w=[hdep])
                ln_rows(k, C, seq, dst, g, tt0, nv, ti, hbuf, hdep, L["gam"], L["bet"], L["cdep"],
                        (L["stats"], L["mv"], L["rstd"], L["hb"], L["d_s"], L["d_hb"], dsh), L["hT_out"], L["d_hTo"], 7)
            k.dma("sp", HTo[:, :, t0:t0 + n], L["hT_out"][:, :, 0:n], L["ds_o"], r=[L["d_hTo"]], w=[seq.dHT[dst][g]])
        seq.cur = dst


def make_invcnt(T):
    t = np.arange(T)
    out = np.zeros((4, T), np.float32)
    for gi, w in enumerate((2, 4, 8, 16)):
        lo = np.clip(t - w // 2, 0, T)
        hi = np.clip(t + w // 2, 0, T)
        out[gi] = 1.0 / (hi - lo).astype(np.float32)
    return out


def load_w(k, dst3, src2, ds, wdep, kchunks, c0=0, c1=None):
    for c in range(kchunks):
        k.dma("pool", dst3[:, c, :], src2[c * 128:(c + 1) * 128, c0:c1], ds, w=[wdep])


def rope_fm(k, out_bf, psa, psb, da, db, cos, sin, dtab, t1, t2, d1, d2, dout, P):
    k.tt("dve", t1, psa, cos, ALU.mult, r=[da, dtab], w=[d1])
    k.tt("dve", t2, psb, sin, ALU.mult, r=[db, dtab], w=[d2])
    k.tt("pool", out_bf, t1, t2, ALU.add, r=[d1, d2], w=[dout])


def swa_sublayer(k, C, seqs, W, li):
    nc = k.nc
    k.sub_begin()
    k.reset_arena(C["keep"])
    L = alloc_ln_state(k, li, 0, W, "sw")
    wq = k.bf(8 * 1536).rearrange("p (c n) -> p c n", c=8)
    wsw = k.bf(8 * 1280).rearrange("p (c n) -> p c n", c=8)
    wo = k.bf(8 * 1024).rearrange("p (c n) -> p c n", c=8)
    wdep = Dep()
    dsw = k.new_dsem("sww")
    wsrc = W["c_w_qkv"][0]
    load_w(k, wq, wsrc, dsw, wdep, 8)
    load_w(k, wo, W["c_w_out"][0], dsw, wdep, 8)
    wsw4 = wsw.rearrange("p c (h two d) -> p c h two d", two=2, d=64)
    for c in range(8):
        srcv = wsrc[c * 128:(c + 1) * 128, 0:1280].rearrange("p (h two d) -> p h two d", two=2, d=64)
        k.dma("pool", wsw4[:, c, :, 0, :], srcv[:, :, 1, :], dsw, w=[wdep])
        k.dma("pool", wsw4[:, c, :, 1, :], srcv[:, :, 0, :], dsw, w=[wdep])
    sk = k.f32(8)
    srow = k.bf(8 * 128).rearrange("p (h n) -> p h n", h=8)
    k.dma("sp", sk[0:1, 0:8], W["c_sink"][0:1, :], dsw, w=[wdep])
    k.act(sk[0:1, 0:8], sk[0:1, 0:8], AF.Exp, r=[wdep], w=[wdep])
    k.copy("dve", srow[0:1, :, :], sk[0:1, 0:8].unsqueeze(2).to_broadcast([1, 8, 128]), r=[wdep], w=[wdep])
    scr = []
    for i, seq in enumerate(seqs):
        scr.append(dict(
            QT=nc.dram_tensor(f"swQT{i}", [1024, seq.T], BF16, kind="Internal").ap(),
            KT=nc.dram_tensor(f"swKT{i}", [256, seq.T], BF16, kind="Internal").ap(),
            V=nc.dram_tensor(f"swV{i}", [seq.T, 256], BF16, kind="Internal").ap()))
    keep2 = k.aoff
    hT_in = k.bf(8 * 512).rearrange("p (c n) -> p c n", c=8)
    tab = k.f32(2 * 512).rearrange("p (c n) -> p c n", c=2)
    t1 = k.f32(512)
    t2 = k.f32(512)
    qk = k.bf(10 * 512).rearrange("p (c n) -> p c n", c=10)
    vt = [k.bf(256) for _ in range(2)]
    d_in, d_tab, d1, d2, d_qk = Dep(), Dep(), Dep(), Dep(), Dep()
    d_vt = [Dep(), Dep()]
    ds_in = k.new_dsem("swi")
    ds_tab = k.new_dsem("swt")
    ds_o = k.new_dsem("swo")
    ds_vs = [k.new_dsem("swv0"), k.new_dsem("swv1")]
    bi = 0
    vi = 0
    for si, seq in enumerate(seqs):
        src = seq.cur
        HTv = seq.HT[src].rearrange("(c p) t -> p c t", p=128)
        QTo = scr[si]["QT"].rearrange("(c p) t -> p c t", p=128)
        KTo = scr[si]["KT"].rearrange("(c p) t -> p c t", p=128)
        for g in range(seq.ngroups):
            t0, n = seq.group(g)
            k.dma("sp", hT_in[:, :, 0:n], HTv[:, :, t0:t0 + n], ds_in, r=[seq.dHT[src][g]], w=[d_in])
            k.dma("sp", tab[:, :, 0:n], C["rope128"][si][:, :, t0:t0 + n].rearrange("c p t -> p c t"), ds_tab, w=[d_tab])
            for h in range(10):
                ba = bi % 4
                bb = (bi + 1) % 4
                bi += 2
                for (b, wt) in ((ba, wq), (bb, wsw)):
                    for kc in range(8):
                        k.mm(k.bank(b, n), wt[:, kc, h * 128:(h + 1) * 128], hT_in[:, kc, 0:n], kc == 0, kc == 7,
                             r=[wdep, d_in], w=[k.pbank[b]], sig=(kc == 7))
                rope_fm(k, qk[:, h, 0:n], k.bank(ba, n), k.bank(bb, n), k.pbank[ba], k.pbank[bb], tab[:, 0, 0:n],
                        tab[:, 1, 0:n], d_tab, t1[:, 0:n], t2[:, 0:n], d1, d2, d_qk, 128)
            k.dma("sp", QTo[:, :, t0:t0 + n], qk[:, 0:8, 0:n], ds_o, r=[d_qk])
            k.dma("sp", KTo[:, :, t0:t0 + n], qk[:, 8:10, 0:n], ds_o, r=[d_qk])
            for ti in range((n + 127) // 128):
                nv = min(128, n - ti * 128)
                tt0 = t0 + ti * 128
                for kc in range(8):
                    k.mm(k.bank(6)[0:nv, 0:256], hT_in[:, kc, ti * 128:ti * 128 + nv], wq[:, kc, 1280:1536], kc == 0, kc == 7,
                         r=[wdep, d_in], w=[k.pbank[6]], sig=(kc == 7))
                v = vt[vi % 2]
                dv = d_vt[vi % 2]
                dsv = ds_vs[vi % 2]
                vi += 1
                k.copy("act", v[0:nv, :], k.bank(6)[0:nv, 0:256], r=[k.pbank[6]], w=[dv])
                k.dma("sp", scr[si]["V"][tt0:tt0 + nv, :], v[0:nv, :], dsv, r=[dv])
    k.barrier()
    k.reset_arena(keep2)
    Qt = k.bf(8 * 128).rearrange("p (c n) -> p c n", c=8)
    Kw = k.bf(2 * 384).rearrange("p (c n) -> p c n", c=2)
    Vw = k.bf(3 * 256).rearrange("p (c n) -> p c n", c=3)
    Pt = [k.bf(128) for _ in range(2)]
    rden = k.f32(128)
    OT = k.bf(8 * 128).rearrange("p (c n) -> p c n", c=8)
    hbufs = [k.f32(D) for _ in range(2)]
    d_q, d_k, d_v, d_rd, d_OT = Dep(), Dep(), Dep(), Dep(), Dep()
    d_P = [Dep(), Dep()]
    d_hbuf = [Dep(), Dep()]
    ds_q = k.new_dsem("swq")
    ds_k = k.new_dsem("swk")
    ds_v = k.new_dsem("swv")
    ds_h = [k.new_dsem(f"swh{i}") for i in range(2)]
    SC = 128 ** -0.5
    pi = 0
    hi = 0
    for si, seq in enumerate(seqs):
        src = seq.cur
        dst = 1 - src
        T = seq.T
        NT = (T + 127) // 128
        QTo = scr[si]["QT"].rearrange("(c p) t -> p c t", p=128)
        KTo = scr[si]["KT"].rearrange("(c p) t -> p c t", p=128)
        HTo = seq.HT[dst].rearrange("(c p) t -> p c t", p=128)
        for i in range(NT):
            q0 = i * 128
            nvq = min(128, T - q0)
            g = i // 4
            ti = i % 4
            t0g, ng = seq.group(g)
            js = [j for j in (i - 1, i, i + 1) if 0 <= j < NT]
            klo = js[0] * 128
            khi = min(T, js[-1] * 128 + 128)
            k.dma("sp", Qt[:, :, 0:nvq], QTo[:, :, q0:q0 + nvq], ds_q, w=[d_q])
            k.dma("sp", Kw[:, :, 0:khi - klo], KTo[:, :, klo:khi], ds_k, w=[d_k])
            for jj, j in enumerate(js):
                nk = min(128, T - j * 128)
                k.dma("sp", Vw[0:nk, jj, :], scr[si]["V"][j * 128:j * 128 + nk, :], ds_v, w=[d_v])
            hbuf = hbufs[hi % 2]
            hdep = d_hbuf[hi % 2]
            dsh = ds_h[hi % 2]
            hi += 1
            k.dma("sp", hbuf[0:nvq, :], seq.H[src][q0:q0 + nvq, :], dsh, r=[seq.dH[src][g]], w=[hdep])
            for h in range(8):
                gq = h // 4
                for jj, j in enumerate(js):
                    nk = min(128, T - j * 128)
                    sbk = pi % 2
                    P = Pt[pi % 2]
                    dP = d_P[pi % 2]
                    pi += 1
                    k.mm(k.bank(sbk)[0:nk, 0:nvq], Kw[:, gq, jj * 128:jj * 128 + nk], Qt[:, h, 0:nvq], True, True,
                         r=[d_k, d_q], w=[k.pbank[sbk]], sig=True)
                    k.act(P[0:nk, 0:nvq], k.bank(sbk)[0:nk, 0:nvq], AF.Exp, scale=SC, r=[k.pbank[sbk]], w=[dP])
                    if j != i:
                        m = C["mprev"] if j < i else C["mnext"]
                        k.tt("pool", P[0:nk, 0:nvq], P[0:nk, 0:nvq], m[0:nk, 0:nvq], ALU.mult, r=[C["dep"]], w=[dP])
                    k.mm(k.bank(2)[:, 0:nvq], Vw[0:nk, jj, gq * 128:(gq + 1) * 128], P[0:nk, 0:nvq], jj == 0,
                         jj == len(js) - 1, r=[d_v, dP], w=[k.pbank[2]], sig=(jj == len(js) - 1))
                    k.mm(k.bank(3)[:, 0:nvq], C["ones"][0:nk, :], P[0:nk, 0:nvq], jj == 0, False,
                         r=[dP, C["dep2"]], w=[k.pbank[3]], sig=False)
                k.mm(k.bank(3)[:, 0:nvq], srow[0:1, h, :], C["ones"][0:1, 0:nvq], False, True, r=[wdep, C["dep2"]],
                     w=[k.pbank[3]], sig=True)
                k.recip(rden[:, 0:nvq], k.bank(3)[:, 0:nvq], r=[k.pbank[3]], w=[d_rd])
                k.tt("dve", OT[:, h, 0:nvq], k.bank(2)[:, 0:nvq], rden[:, 0:nvq], ALU.mult, r=[k.pbank[2], d_rd], w=[d_OT])
            y_ps = k.psum[:, 4 * 512:4 * 512 + 1024]
            for half in range(2):
                for h in range(8):
                    k.mm(k.bank(4 + half)[0:nvq, :], OT[:, h, 0:nvq], wo[:, h, half * 512:(half + 1) * 512], h == 0, h == 7,
                         r=[wdep, d_OT], w=[k.pbank[4 + half]], sig=(h == 7))
            ln_epilogue(k, C, seq, dst, g, q0, nvq, ti, y_ps, [k.pbank[4], k.pbank[5]], hbuf, hdep, L["gam"], L["bet"],
                        L["cdep"], (L["stats"], L["mv"], L["rstd"], L["hb"], L["d_s"], L["d_hb"], dsh), L["hT_out"],
                        L["d_hTo"], 7)
            if ti == 3 or i == NT - 1:
                k.dma("sp", HTo[:, :, t0g:t0g + ng], L["hT_out"][:, :, 0:ng], L["ds_o"], r=[L["d_hTo"]],
                      w=[seq.dHT[dst][g]])
        seq.cur = dst


def make_rope_fm(T, d):
    inv = 1.0 / (10000.0 ** (np.arange(0, d, 2, dtype=np.float32) / d))
    ang = np.arange(T, dtype=np.float32)[None, :] * inv[:, None].astype(np.float32)
    cos = np.cos(ang).astype(np.float32)
    sin = np.sin(ang).astype(np.float32)
    out = np.zeros((2, d, T), np.float32)
    out[0, :d // 2] = cos
    out[0, d // 2:] = cos
    out[1, :d // 2] = -sin
    out[1, d // 2:] = sin
    return out


def mla_sublayer(k, C, seqs, W, li):
    nc = k.nc
    k.sub_begin()
    k.reset_arena(C["keep"])
    L = alloc_ln_state(k, li, 0, W, "ml")
    Tmax = max(seq.T for seq in seqs)
    cqn = k.bf(2 * Tmax).rearrange("p (c n) -> p c n", c=2)
    cn = k.bf(2 * Tmax).rearrange("p (c n) -> p c n", c=2)
    krT = k.bf(Tmax)
    gq = k.f32(8)
    gkv = k.f32(8)
    gdep = Dep()
    dsg = k.new_dsem("mlg")
    for c in range(2):
        k.dma("sp", gq[:, c:c + 1], W["d_q_norm_g"][0:1, c * 128:(c + 1) * 128].rearrange("o p -> p o"), dsg, w=[gdep])
        k.dma("sp", gkv[:, c:c + 1], W["d_kv_norm_g"][0:1, c * 128:(c + 1) * 128].rearrange("o p -> p o"), dsg, w=[gdep])
    d_cqn, d_cn, d_kr = Dep(), Dep(), Dep()
    keep2 = k.aoff
    SC = 192 ** -0.5
    for si, seq in enumerate(seqs):
        T = seq.T
        NT = (T + 127) // 128
        src = seq.cur
        dst = 1 - src
        HTv = seq.HT[src].rearrange("(c p) t -> p c t", p=128)
        HTo = seq.HT[dst].rearrange("(c p) t -> p c t", p=128)
        OTs = nc.dram_tensor(f"mlOT{si}", [2048, T], BF16, kind="Internal").ap()
        OTv = OTs.rearrange("(c p) t -> p c t", p=128)
        k.barrier()
        k.reset_arena(keep2)
        wdq = k.bf(8 * 256).rearrange("p (c n) -> p c n", c=8)
        wdkv = k.bf(8 * 320).rearrange("p (c n) -> p c n", c=8)
        wkrs = k.bf(8 * 64).rearrange("p (c n) -> p c n", c=8)
        wdep = Dep()
        dsw = k.new_dsem(f"mlw1_{si}")
        load_w(k, wdq, W["d_w_dq"][0], dsw, wdep, 8)
        load_w(k, wdkv, W["d_w_dkv"][0], dsw, wdep, 8)
        for c in range(8):
            k.dma("pool", wkrs[:, c, 0:32], W["d_w_dkv"][0, c * 128:(c + 1) * 128, 288:320], dsw, w=[wdep])
            k.dma("pool", wkrs[:, c, 32:64], W["d_w_dkv"][0, c * 128:(c + 1) * 128, 256:288], dsw, w=[wdep])
        hT_in = k.bf(8 * 512).rearrange("p (c n) -> p c n", c=8)
        sq = k.bf(2 * 512).rearrange("p (c n) -> p c n", c=2)
        rs = k.f32(512)
        tab = k.f32(2 * 512).rearrange("p (c n) -> p c n", c=2)
        t1 = k.f32(512)
        t2 = k.f32(512)
        d_in, d_sq, d_rs, d_tab, d1, d2 = Dep(), Dep(), Dep(), Dep(), Dep(), Dep()
        ds_in = k.new_dsem(f"mli_{si}")
        ds_tab = k.new_dsem(f"mlt1_{si}")
        for g in range(seq.ngroups):
            t0, n = seq.group(g)
            k.dma("sp", hT_in[:, :, 0:n], HTv[:, :, t0:t0 + n], ds_in, r=[seq.dHT[src][g]], w=[d_in])
            k.dma("sp", tab[0:64, :, 0:n], C["rope64"][si][:, :, t0:t0 + n].rearrange("c p t -> p c t"), ds_tab, w=[d_tab])
            for (wt, gcol, outb, dout) in ((wdq, gq, cqn, d_cqn), (wdkv, gkv, cn, d_cn)):
                for c in range(2):
                    for kc in range(8):
                        k.mm(k.bank(c, n), wt[:, kc, c * 128:(c + 1) * 128], hT_in[:, kc, 0:n], kc == 0, kc == 7,
                             r=[wdep, d_in], w=[k.pbank[c]], sig=(kc == 7))
                    k.act(sq[:, c, 0:n], k.bank(c, n), AF.Square, r=[k.pbank[c]], w=[d_sq])
                for c in range(2):
                    k.mm(k.bank(2, n), C["ones"][:, :], sq[:, c, 0:n], c == 0, c == 1, r=[d_sq, C["dep2"]], w=[k.pbank[2]],
                         sig=(c == 1))
                k.ts("dve", rs[:, 0:n], k.bank(2, n), 1.0 / 256.0, float(RMS_EPS), ALU.mult, ALU.add, r=[k.pbank[2]], w=[d_rs])
                k.tt("pool", rs[:, 0:n], rs[:, 0:n], C["neghalf"][:, 0:1].to_broadcast([128, n]), ALU.pow, r=[C["dep2"]],
                     w=[d_rs])
                for c in range(2):
                    k.stt("dve", outb[:, c, t0:t0 + n], k.bank(c, n), gcol[:, c:c + 1], rs[:, 0:n], ALU.mult, ALU.mult,
                          r=[k.pbank[c], d_rs, gdep], w=[dout])
            for (b, wt, c0) in ((4, wdkv, 256), (5, wkrs, 0)):
                for kc in range(8):
                    k.mm(k.bank(b)[0:64, 0:n], wt[:, kc, c0:c0 + 64], hT_in[:, kc, 0:n], kc == 0, kc == 7, r=[wdep, d_in],
                         w=[k.pbank[b]], sig=(kc == 7))
            rope_fm(k, krT[0:64, t0:t0 + n], k.bank(4)[0:64, 0:n], k.bank(5)[0:64, 0:n], k.pbank[4], k.pbank[5],
                    tab[0:64, 0, 0:n], tab[0:64, 1, 0:n], d_tab, t1[0:64, 0:n], t2[0:64, 0:n], d1, d2, d_kr, 64)
        k.barrier()
        k.reset_arena(keep2)
        wuq = k.bf(2 * 3072).rearrange("p (c n) -> p c n", c=2)
        wuqs = k.bf(2 * 1024).rearrange("p (c n) -> p c n", c=2)
        wukv = k.bf(2 * 4096).rearrange("p (c n) -> p c n", c=2)
        wdep = Dep()
        dsw = k.new_dsem(f"mlw2_{si}")
        load_w(k, wuq, W["d_w_uq"][0], dsw, wdep, 2)
        load_w(k, wukv, W["d_w_ukv"][0], dsw, wdep, 2)
        wuqs4 = wuqs.rearrange("p c (h two d) -> p c h two d", two=2, d=32)
        for c in range(2):
            srcv = W["d_w_uq"][0, c * 128:(c + 1) * 128, :].rearrange("p (h e) -> p h e", e=192)
            k.dma("pool", wuqs4[:, c, :, 0, :], srcv[:, :, 160:192], dsw, w=[wdep])
            k.dma("pool", wuqs4[:, c, :, 1, :], srcv[:, :, 128:160], dsw, w=[wdep])
        KnT = k.bf(Tmax)
        Vh = k.bf(NT * 128).rearrange("p (j d) -> p j d", d=128)
        QnT = k.bf(512)
        QrT = k.bf(512)
        Pt = [k.bf(512) for _ in range(2)]
        rden = k.f32(512)
        OTg = k.bf(512)
        tab = k.f32(2 * 512).rearrange("p (c n) -> p c n", c=2)
        t1 = k.f32(512)
        t2 = k.f32(512)
        d_Kn, d_Vh, d_Qn, d_Qr, d_rd, d_OTg, d_tab, d1, d2 = (Dep() for _ in range(9))
        d_P = [Dep(), Dep()]
        ds_t = k.new_dsem(f"mlt_{si}")
        ds_ot = k.new_dsem(f"mlot_{si}")
        pi = 0
        kb = 0
        for h in range(16):
            for g in range(seq.ngroups):
                t0, n = seq.group(g)
                b = 6 + (kb % 2)
                kb += 1
                for kc in range(2):
                    k.mm(k.bank(b, n), wukv[:, kc, h * 256:h * 256 + 128], cn[:, kc, t0:t0 + n], kc == 0, kc == 1,
                         r=[wdep, d_cn], w=[k.pbank[b]], sig=(kc == 1))
                k.copy("act", KnT[:, t0:t0 + n], k.bank(b, n), r=[k.pbank[b]], w=[d_Kn])
            for j in range(NT):
                nk = min(128, T - j * 128)
                b = 6 + (kb % 2)
                kb += 1
                for kc in range(2):
                    k.mm(k.bank(b)[0:nk, 0:128], cn[:, kc, j * 128:j * 128 + nk], wukv[:, kc, h * 256 + 128:h * 256 + 256],
                         kc == 0, kc == 1, r=[wdep, d_cn], w=[k.pbank[b]], sig=(kc == 1))
                k.copy("dve", Vh[0:nk, j, :], k.bank(b)[0:nk, 0:128], r=[k.pbank[b]], w=[d_Vh])
            for g in range(seq.ngroups):
                t0, n = seq.group(g)
                k.dma("sp", tab[0:64, :, 0:n], C["rope64"][si][:, :, t0:t0 + n].rearrange("c p t -> p c t"), ds_t, w=[d_tab])
                for kc in range(2):
                    k.mm(k.bank(4, n), wuq[:, kc, h * 192:h * 192 + 128], cqn[:, kc, t0:t0 + n], kc == 0, kc == 1,
                         r=[wdep, d_cqn], w=[k.pbank[4]], sig=(kc == 1))
                k.copy("act", QnT[:, 0:n], k.bank(4, n), r=[k.pbank[4]], w=[d_Qn])
                for (b, wt, c0) in ((5, wuq, h * 192 + 128), (6 + (kb % 2), wuqs, h * 64)):
                    for kc in range(2):
                        k.mm(k.bank(b)[0:64, 0:n], wt[:, kc, c0:c0 + 64], cqn[:, kc, t0:t0 + n], kc == 0, kc == 1,
                             r=[wdep, d_cqn], w=[k.pbank[b]], sig=(kc == 1))
                bsw = 6 + (kb % 2)
                kb += 1
                rope_fm(k, QrT[0:64, 0:n], k.bank(5)[0:64, 0:n], k.bank(bsw)[0:64, 0:n], k.pbank[5], k.pbank[bsw],
                        tab[0:64, 0, 0:n], tab[0:64, 1, 0:n], d_tab, t1[0:64, 0:n], t2[0:64, 0:n], d1, d2, d_Qr, 64)
                for j in range(NT):
                    nk = min(128, T - j * 128)
                    sbk = pi % 2
                    P = Pt[pi % 2]
                    dP = d_P[pi % 2]
                    pi += 1
                    k.mm(k.bank(sbk)[0:nk, 0:n], KnT[:, j * 128:j * 128 + nk], QnT[:, 0:n], True, False,
                         r=[d_Kn, d_Qn], w=[k.pbank[sbk]], sig=False)
                    k.mm(k.bank(sbk)[0:nk, 0:n], krT[0:64, j * 128:j * 128 + nk], QrT[0:64, 0:n], False, True,
                         r=[d_kr, d_Qr], w=[k.pbank[sbk]], sig=True)
                    k.act(P[0:nk, 0:n], k.bank(sbk)[0:nk, 0:n], AF.Exp, scale=SC, r=[k.pbank[sbk]], w=[dP])
                    k.mm(k.bank(2, n), Vh[0:nk, j, :], P[0:nk, 0:n], j == 0, j == NT - 1, r=[d_Vh, dP], w=[k.pbank[2]],
                         sig=(j == NT - 1))
                    k.mm(k.bank(3, n), C["ones"][0:nk, :], P[0:nk, 0:n], j == 0, j == NT - 1, r=[dP, C["dep2"]],
                         w=[k.pbank[3]], sig=(j == NT - 1))
                k.recip(rden[:, 0:n], k.bank(3, n), r=[k.pbank[3]], w=[d_rd])
                k.tt("dve", OTg[:, 0:n], k.bank(2, n), rden[:, 0:n], ALU.mult, r=[k.pbank[2], d_rd], w=[d_OTg])
                k.dma("sp", OTs[h * 128:(h + 1) * 128, t0:t0 + n], OTg[:, 0:n], ds_ot, r=[d_OTg])
        k.barrier()
        k.reset_arena(keep2)
        wout = k.bf(16 * 1024).rearrange("p (c n) -> p c n", c=16)
        wdep = Dep()
        dsw = k.new_dsem(f"mlw3_{si}")
        load_w(k, wout, W["d_w_out"][0], dsw, wdep, 16)
        OT = k.bf(16 * 512).rearrange("p (c n) -> p c n", c=16)
        hbufs = [k.f32(D) for _ in range(2)]
        d_OT = Dep()
        d_hbuf = [Dep(), Dep()]
        ds_q = k.new_dsem(f"mlq_{si}")
        ds_h = [k.new_dsem(f"mlh{i}_{si}") for i in range(2)]
        hi = 0
        yi = 0
        y_banks = [(0, 1), (2, 3)]
        for g in range(seq.ngroups):
            t0, n = seq.group(g)
            k.dma("sp", OT[:, :, 0:n], OTv[:, :, t0:t0 + n], ds_q, w=[d_OT])
            for ti in range((n + 127) // 128):
                nv = min(128, n - ti * 128)
                tt0 = t0 + ti * 128
                hbuf = hbufs[hi % 2]
                hdep = d_hbuf[hi % 2]
                dsh = ds_h[hi % 2]
                hi += 1
                k.dma("sp", hbuf[0:nv, :], seq.H[src][tt0:tt0 + nv, :], dsh, r=[seq.dH[src][g]], w=[hdep])
                yb = y_banks[yi % 2]
                yi += 1
                y_ps = k.psum[:, yb[0] * 512:yb[0] * 512 + 1024]
                for half in range(2):
                    for h in range(16):
                        k.mm(k.bank(yb[half])[0:nv, :], OT[:, h, ti * 128:ti * 128 + nv], wout[:, h, half * 512:(half + 1) * 512],
                             h == 0, h == 15, r=[wdep, d_OT], w=[k.pbank[yb[half]]], sig=(h == 15))
                ln_epilogue(k, C, seq, dst, g, tt0, nv, ti, y_ps, [k.pbank[yb[0]], k.pbank[yb[1]]], hbuf, hdep, L["gam"],
                            L["bet"], L["cdep"], (L["stats"], L["mv"], L["rstd"], L["hb"], L["d_s"], L["d_hb"], dsh),
                            L["hT_out"], L["d_hTo"], 7)
            k.dma("sp", HTo[:, :, t0:t0 + n], L["hT_out"][:, :, 0:n], L["ds_o"], r=[L["d_hTo"]], w=[seq.dHT[dst][g]])
        seq.cur = dst


def hgrn2_sublayer(k, C, seqs, W, li):
    nc = k.nc
    k.sub_begin()
    k.reset_arena(C["keep"])
    CH = 32
    NB = 256
    L = alloc_ln_state(k, li, 0, W, "hg")
    lg = k.f32(40).rearrange("p (r h) -> p r h", r=5)
    LB = k.f32(8)
    OML = k.f32(8)
    ssum = k.f32(8)
    ng = k.f32(128)
    dlb = Dep()
    dsl = k.new_dsem("hgl")
    for r in range(5):
        for h in range(8):
            k.dma("sp", lg[:, r, h:h + 1], W["hg_lb_logits"][r:r + 1, h * 128:(h + 1) * 128].rearrange("o p -> p o"), dsl,
                  w=[dlb])
    k.dma("sp", ng, W["a_norm_g"][0:1, :].partition_broadcast(128), dsl, w=[dlb])
    k.act(lg[:, :, :], lg[:, :, :], AF.Exp, r=[dlb], w=[dlb])
    k.tt("dve", ssum[:, 0:8], lg[:, 0, :], lg[:, 1, :], ALU.add, r=[dlb], w=[dlb])
    for r in range(2, li + 1):
        pass
    for r in range(2, 5):
        k.tt("dve", ssum[:, 0:8], ssum[:, 0:8], lg[:, r, :], ALU.add, r=[dlb], w=[dlb])
    k.recip(ssum[:, 0:8], ssum[:, 0:8], r=[dlb], w=[dlb])
    k.tt("dve", LB[:, 0:8], lg[:, 0, :], ssum[:, 0:8], ALU.mult, r=[dlb], w=[dlb])
    k.ts("dve", OML[:, 0:8], LB[:, 0:8], -1.0, 1.0, ALU.mult, ALU.add, r=[dlb], w=[dlb])
    win = k.bf(8 * 4096).rearrange("p (c n) -> p c n", c=8)
    wout = k.bf(8 * 1024).rearrange("p (c n) -> p c n", c=8)
    wdep = Dep()
    hT_in = k.bf(8 * NB).rearrange("p (c n) -> p c n", c=8)
    B1 = k.f32(8 * NB).rearrange("p (h n) -> p h n", h=8)
    B2 = k.f32(8 * NB).rearrange("p (h n) -> p h n", h=8)
    B3 = k.f32(8 * NB).rearrange("p (h n) -> p h n", h=8)
    B4 = k.f32(8 * NB).rearrange("p (h n) -> p h n", h=8)
    qe = k.bf(8 * NB).rearrange("p (h n) -> p h n", h=8)
    ke = k.bf(8 * NB).rearrange("p (h n) -> p h n", h=8)
    kd = k.bf(8 * NB).rearrange("p (h n) -> p h n", h=8)
    S = k.f32(1024).rearrange("p (h e) -> p h e", h=8)
    Sb = k.bf(1024).rearrange("p (h e) -> p h e", h=8)
    Vc = k.bf(1024)
    kdt = k.bf(1024)
    At = k.bf(256).rearrange("p (h t) -> p h t", h=8)
    osb = [k.f32(1024) for _ in range(2)]
    ofb = k.f32(1024)
    sqb = k.f32(1024)
    sgl = k.f32(1024)
    ss = k.f32(8)
    ob = k.bf(1024)
    oT = k.bf(8 * 512).rearrange("p (c n) -> p c n", c=8)
    hbufs = [k.f32(D) for _ in range(2)]
    d_in, d1, d2, d3, d4, d_qe, d_ke, d_kd, d_S, d_Sb, d_Vc, d_kdt, d_At = (Dep() for _ in range(13))
    d_osb = [Dep(), Dep()]
    d_of, d_sq, d_sgl, d_ss, d_ob, d_oT = (Dep() for _ in range(6))
    d_hbuf = [Dep(), Dep()]
    ds_w = k.new_dsem("hgw")
    ds_in = k.new_dsem("hgi")
    ds_os = [k.new_dsem("hgo0"), k.new_dsem("hgo1")]
    ds_of = k.new_dsem("hgof")
    ds_h = [k.new_dsem("hgh0"), k.new_dsem("hgh1")]
    OF = [nc.dram_tensor(f"hgOF{i}", [seq.T, D], F32, kind="Internal").ap() for i, seq in enumerate(seqs)]
    d_OF = [Dep() for _ in seqs]
    cnt = dict(pb=0, os=0, hi=0)

    def proj_fm(col0, n):
        b = cnt["pb"] % 2
        cnt["pb"] += 1
        for kc in range(8):
            k.mm(k.bank(b, n), win[:, kc, col0:col0 + 128], hT_in[:, kc, 0:n], kc == 0, kc == 7, r=[wdep, d_in],
                 w=[k.pbank[b]], sig=(kc == 7))
        return b

    def proj_tm(col0, c0, cs, banks):
        for half in range(2):
            for kc in range(8):
                k.mm(k.bank(banks[half])[0:cs, :], hT_in[:, kc, c0:c0 + cs], win[:, kc, col0 + half * 512:col0 + (half + 1) * 512],
                     kc == 0, kc == 7, r=[wdep, d_in], w=[k.pbank[banks[half]]], sig=(kc == 7))

    for direction in ("fwd", "bwd"):
        fwd = direction == "fwd"
        k.barrier()
        load_w(k, win[:, :, 0:2048], W["a_w_in"][0], ds_w, wdep, 8, 0, 2048)
        if fwd:
            load_w(k, win[:, :, 2048:3072], W["a_w_in"][0], ds_w, wdep, 8, 2048, 3072)
        else:
            load_w(k, win[:, :, 2048:4096], W["a_w_in"][0], ds_w, wdep, 8, 3072, 5120)
            load_w(k, wout, W["a_w_out"][0], ds_w, wdep, 8)
        mask = C["mnext"] if fwd else C["mprev"]
        for si, seq in enumerate(seqs):
            T = seq.T
            src = seq.cur
            dst = 1 - src
            HTv = seq.HT[src].rearrange("(c p) t -> p c t", p=128)
            HTo = seq.HT[dst].rearrange("(c p) t -> p c t", p=128)
            k.memset("dve", S[:, :, :], 0.0, w=[d_S])
            k.memset("pool", Sb[:, :, :], 0.0, w=[d_Sb])
            groups = list(range(seq.ngroups))
            if not fwd:
                groups = groups[::-1]
            for g in groups:
                tg0, ng_ = seq.group(g)
                halves = [(tg0 + o, min(NB, ng_ - o)) for o in range(0, ng_, NB)]
                if not fwd:
                    halves = halves[::-1]
                for (t0, n) in halves:
                    ch = min(CH, n)
                    nch = n // ch
                    k.dma("sp", hT_in[:, :, 0:n], HTv[:, :, t0:t0 + n], ds_in, r=[seq.dHT[src][g]], w=[d_in])
                    for h in range(8):
                        b = proj_fm(h * 128, n)
                        k.act(B4[:, h, 0:n], k.bank(b, n), AF.Silu, r=[k.pbank[b]], w=[d4])
                    for h in range(8):
                        b = proj_fm(2048 + h * 128, n)
                        k.act(B1[:, h, 0:n], k.bank(b, n), AF.Sigmoid, r=[k.pbank[b]], w=[d1])
                    k.tt("dve", B1[:, :, 0:n], B1[:, :, 0:n], OML[:, 0:8].unsqueeze(2).to_broadcast([128, 8, n]), ALU.mult,
                         r=[dlb], w=[d1])
                    k.tt("dve", B1[:, :, 0:n], B1[:, :, 0:n], LB[:, 0:8].unsqueeze(2).to_broadcast([128, 8, n]), ALU.add,
                         r=[dlb], w=[d1])
                    k.act(B2[:, :, 0:n], B1[:, :, 0:n], AF.Ln, r=[d1], w=[d2])
                    k.ts("dve", B1[:, :, 0:n], B1[:, :, 0:n], -1.0, 1.0, ALU.mult, ALU.add, w=[d1])
                    cur, dcur, oth, doth = B2, d2, B3, d3
                    sh = 1
                    while sh < ch:
                        c4 = cur[:, :, 0:n].rearrange("p h (c t) -> p h c t", t=ch)
                        o4 = oth[:, :, 0:n].rearrange("p h (c t) -> p h c t", t=ch)
                        for h in range(8):
                            if fwd:
                                k.tt("dve", o4[:, h, :, sh:ch], c4[:, h, :, sh:ch], c4[:, h, :, 0:ch - sh], ALU.add, r=[dcur],
                                     w=[doth])
                            else:
                                k.tt("dve", o4[:, h, :, 0:ch - sh], c4[:, h, :, 0:ch - sh], c4[:, h, :, sh:ch], ALU.add,
                                     r=[dcur], w=[doth])
                        for h in range(8):
                            if fwd:
                                k.copy("pool", o4[:, h, :, 0:sh], c4[:, h, :, 0:sh], r=[dcur], w=[doth])
                            else:
                                k.copy("pool", o4[:, h, :, ch - sh:ch], c4[:, h, :, ch - sh:ch], r=[dcur], w=[doth])
                        cur, dcur, oth, doth = oth, doth, cur, dcur
                        sh *= 2
                    bb, dbb, eb, deb = cur, dcur, oth, doth
                    Lidx = ch - 1 if fwd else 0
                    k.act(eb[:, :, 0:n], bb[:, :, 0:n], AF.Exp, r=[dbb], w=[deb])
                    k.tt("dve", qe[:, :, 0:n], B4[:, :, 0:n], eb[:, :, 0:n], ALU.mult, r=[d4, deb], w=[d_qe])
                    k.act(B4[:, :, 0:n], bb[:, :, 0:n], AF.Exp, scale=-1.0, r=[dbb], w=[d4])
                    k.tt("dve", ke[:, :, 0:n], B1[:, :, 0:n], B4[:, :, 0:n], ALU.mult, r=[d1, d4], w=[d_ke])
                    b4 = bb[:, :, 0:n].rearrange("p h (c t) -> p h c t", t=ch)
                    t4 = B4[:, :, 0:n].rearrange("p h (c t) -> p h c t", t=ch)
                    for h in range(8):
                        k.tt("dve", t4[:, h, :, :], b4[:, h, :, Lidx:Lidx + 1].to_broadcast([128, nch, ch]), b4[:, h, :, :],
                             ALU.subtract, r=[dbb], w=[d4])
                    k.act(B4[:, :, 0:n], B4[:, :, 0:n], AF.Exp, w=[d4])
                    k.tt("dve", kd[:, :, 0:n], B1[:, :, 0:n], B4[:, :, 0:n], ALU.mult, r=[d1, d4], w=[d_kd])
                    e4 = eb[:, :, 0:n].rearrange("p h (c t) -> p h c t", t=ch)
                    chunks = list(range(nch))
                    if not fwd:
                        chunks = chunks[::-1]
                    for c in chunks:
                        c0 = c * ch
                        tc0 = t0 + c0
                        cs = ch
                        proj_tm(1024, c0, cs, (2, 3))
                        k.copy("act", Vc[0:cs, :], k.psum[0:cs, 2 * 512:2 * 512 + 1024], r=[k.pbank[2], k.pbank[3]], w=[d_Vc])
                        for h in range(8):
                            k.mm(k.bank(4)[0:cs, h * 32:h * 32 + cs], ke[:, h, c0:c0 + cs], qe[:, h, c0:c0 + cs], True, True,
                                 r=[d_ke, d_qe], w=[k.pbank[4]], sig=(h == 7))
                        k.stt("dve", At[0:cs, :, 0:cs], k.bank(4)[0:cs, 0:256].rearrange("p (h t) -> p h t", h=8)[:, :, 0:cs],
                              1e30, mask[0:cs, 0:cs].unsqueeze(1).to_broadcast([cs, 8, cs]), ALU.min, ALU.mult,
                              r=[k.pbank[4], C["dep"]], w=[d_At])
                        for h in range(8):
                            ob_ = k.bank(5 + h // 4)[0:cs, (h % 4) * 128:(h % 4) * 128 + 128]
                            k.mm(ob_, At[0:cs, h, 0:cs], Vc[0:cs, h * 128:(h + 1) * 128], True, False, r=[d_At, d_Vc],
                                 w=[k.pbank[5 + h // 4]], sig=False)
                            k.mm(ob_, qe[:, h, c0:c0 + cs], Sb[:, h, :], False, True, r=[d_qe, d_Sb],
                                 w=[k.pbank[5 + h // 4]], sig=(h % 4 == 3))
                        for h in range(8):
                            k.mm(k.bank(h // 4)[0:cs, (h % 4) * 128:(h % 4) * 128 + 128], kd[:, h, c0:c0 + cs], C["ident"][:, :],
                                 True, True, r=[d_kd, C["dep"]], w=[k.pbank[h // 4]], sig=(h % 4 == 3))
                        k.copy("dve", kdt[0:cs, :], k.psum[0:cs, 0:1024], r=[k.pbank[0], k.pbank[1]], w=[d_kdt])
                        for h in range(8):
                            k.mm(k.bank(2 + h // 4)[:, (h % 4) * 128:(h % 4) * 128 + 128], kdt[0:cs, h * 128:(h + 1) * 128],
                                 Vc[0:cs, h * 128:(h + 1) * 128], True, True, r=[d_kdt, d_Vc], w=[k.pbank[2 + h // 4]],
                                 sig=(h % 4 == 3))
                        k.tt("dve", S[:, :, :], S[:, :, :], e4[:, :, c, Lidx:Lidx + 1].to_broadcast([128, 8, 128]), ALU.mult,
                             r=[deb], w=[d_S])
                        k.tt("dve", S[:, :, :], S[:, :, :], k.psum[:, 2 * 512:2 * 512 + 1024].rearrange("p (h e) -> p h e", h=8),
                             ALU.add, r=[k.pbank[2], k.pbank[3]], w=[d_S])
                        k.copy("act", Sb[:, :, :], S[:, :, :], r=[d_S], w=[d_Sb])
                        o_ps = k.psum[0:cs, 5 * 512:5 * 512 + 1024]
                        if fwd:
                            o_ = osb[cnt["os"] % 2]
                            do_ = d_osb[cnt["os"] % 2]
                            dso = ds_os[cnt["os"] % 2]
                            cnt["os"] += 1
                            k.copy("act", o_[0:cs, :], o_ps, r=[k.pbank[5], k.pbank[6]], w=[do_])
                            k.dma("sp", OF[si][tc0:tc0 + cs, :], o_[0:cs, :], dso, r=[do_], w=[d_OF[si]])
                            continue
                        k.dma("sp", ofb[0:cs, :], OF[si][tc0:tc0 + cs, :], ds_of, r=[d_OF[si]], w=[d_of])
                        k.tt("dve", ofb[0:cs, :], ofb[0:cs, :], o_ps, ALU.add, r=[k.pbank[5], k.pbank[6]], w=[d_of])
                        k.tt("pool", sqb[0:cs, :], ofb[0:cs, :], ofb[0:cs, :], ALU.mult, r=[d_of], w=[d_sq])
                        k.op("dve", (lambda ss_=ss[0:cs, 0:8], sq_=sqb[0:cs, :].rearrange("p (h e) -> p h e", h=8):
                                     (lambda e: e.tensor_reduce(out=ss_, in_=sq_, axis=mybir.AxisListType.X, op=ALU.add)))(),
                             r=[d_sq], w=[d_ss])
                        k.ts("dve", ss[0:cs, 0:8], ss[0:cs, 0:8], 1.0 / 128.0, float(RMS_EPS), ALU.mult, ALU.add, w=[d_ss])
                        k.tt("pool", ss[0:cs, 0:8], ss[0:cs, 0:8], C["neghalf"][0:cs, 0:1].to_broadcast([cs, 8]), ALU.pow,
                             r=[C["dep2"]], w=[d_ss])
                        o3 = ofb[0:cs, :].rearrange("p (h e) -> p h e", h=8)
                        k.tt("dve", o3, o3, ss[0:cs, 0:8].unsqueeze(2).to_broadcast([cs, 8, 128]), ALU.mult, r=[d_ss], w=[d_of])
                        k.tt("pool", o3, o3, ng[0:cs, :].unsqueeze(1).to_broadcast([cs, 8, 128]), ALU.mult, r=[dlb], w=[d_of])
                        proj_tm(3072, c0, cs, (0, 1))
                        k.act(sgl[0:cs, :], k.psum[0:cs, 0:1024], AF.Silu, r=[k.pbank[0], k.pbank[1]], w=[d_sgl])
                        k.tt("dve", ob[0:cs, :], ofb[0:cs, :], sgl[0:cs, :], ALU.mult, r=[d_of, d_sgl], w=[d_ob])
                        tloc = tc0 - tg0
                        for c8 in range(8):
                            k.mm(k.bank(4)[:, c8 * 32:c8 * 32 + cs], ob[0:cs, c8 * 128:(c8 + 1) * 128], C["ident"][0:cs, 0:cs],
                                 True, True, r=[d_ob, C["dep"]], w=[k.pbank[4]], sig=(c8 == 7))
                        k.copy("act", oT[:, :, tloc:tloc + cs],
                               k.bank(4)[:, 0:256].rearrange("p (c t) -> p c t", c=8)[:, :, 0:cs], r=[k.pbank[4]], w=[d_oT])
                        if tloc % 128 == 0:
                            ti = tloc // 128
                            nv = min(128, ng_ - tloc)
                            hbuf = hbufs[cnt["hi"] % 2]
                            hdep = d_hbuf[cnt["hi"] % 2]
                            dsh = ds_h[cnt["hi"] % 2]
                            cnt["hi"] += 1
                            k.dma("sp", hbuf[0:nv, :], seq.H[src][tc0:tc0 + nv, :], dsh, r=[seq.dH[src][g]], w=[hdep])
                            y_ps = k.psum[:, 5 * 512:5 * 512 + 1024]
                            for half in range(2):
                                for c8 in range(8):
                                    k.mm(k.bank(5 + half)[0:nv, :], oT[:, c8, tloc:tloc + nv], wout[:, c8, half * 512:(half + 1) * 512],
                                         c8 == 0, c8 == 7, r=[wdep, d_oT], w=[k.pbank[5 + half]], sig=(c8 == 7))
                            ln_epilogue(k, C, seq, dst, g, tc0, nv, ti, y_ps, [k.pbank[5], k.pbank[6]], hbuf, hdep, L["gam"],
                                        L["bet"], L["cdep"], (L["stats"], L["mv"], L["rstd"], L["hb"], L["d_s"], L["d_hb"], dsh),
                                        L["hT_out"], L["d_hTo"], 7)
                if not fwd:
                    k.dma("sp", HTo[:, :, tg0:tg0 + ng_], L["hT_out"][:, :, 0:ng_], L["ds_o"], r=[L["d_hTo"]],
                          w=[seq.dHT[dst][g]])
            if not fwd:
                seq.cur = dst


def prologue(k, C, seqs, xs, meta):
    k.sub_begin()
    k.reset_arena(C["keep"])
    xb = [k.f32(D) for _ in range(2)]
    hb = k.bf(D)
    hT_out = k.bf(8 * 512).rearrange("p (c n) -> p c n", c=8)
    dx = [Dep(), Dep()]
    d_hb, d_hTo = Dep(), Dep()
    ds = [k.new_dsem(f"pro{i}") for i in range(2)]
    ds_o = k.new_dsem("pro_o")
    i = 0
    for seq, x in zip(seqs, xs):
        HTo = seq.HT[0].rearrange("(c p) t -> p c t", p=128)
        for g in range(seq.ngroups):
            t0, n = seq.group(g)
            ntile = (n + 127) // 128
            for ti in range(ntile):
                nv = min(128, n - ti * 128)
                tt0 = t0 + ti * 128
                b = xb[i % 2]
                d = dx[i % 2]
                dsx = ds[i % 2]
                i += 1
                if tt0 == 0:
                    k.dma("sp", b[0:N_META, :], meta[:, :], dsx, w=[d])
                    k.dma("sp", b[N_META:nv, :], x[0:nv - N_META, :], dsx, w=[d])
                else:
                    k.dma("sp", b[0:nv, :], x[tt0 - N_META:tt0 - N_META + nv, :], dsx, w=[d])
                k.dma("sp", seq.H[0][tt0:tt0 + nv, :], b[0:nv, :], dsx, r=[d], w=[seq.dH[0][g]])
                k.copy("act", hb[0:nv, :], b[0:nv, :], r=[d], w=[d_hb])
                transpose_rows(k, C, hb, d_hb, nv, hT_out, d_hTo, ti * 128, 7)
            k.dma("sp", HTo[:, :, t0:t0 + n], hT_out[:, :, 0:n], ds_o, r=[d_hTo], w=[seq.dHT[0][g]])
        seq.cur = 0


def epilogue_out(k, seqs, outs):
    ds = k.new_dsem("outs")
    for seq, o in zip(seqs, outs):
        deps = seq.dH[seq.cur]
        n = seq.T - N_META
        step = 1024
        for r0 in range(0, n, step):
            r1 = min(n, r0 + step)
            k.dma("sp", o[r0:r1, :], seq.H[seq.cur][N_META + r0:N_META + r1, :], ds, r=deps)


CONST_COLS = 384


def make_consts():
    c = np.zeros((128, CONST_COLS), np.float32)
    c[:, 0:128] = np.eye(128, dtype=np.float32)
    p = np.arange(128)[:, None]
    f = np.arange(128)[None, :]
    c[:, 128:256] = (f <= p)
    c[:, 256:384] = (p <= f)
    return c


WNAMES = ["meta_tokens", "hg_lb_logits", "a_w_in", "a_w_out", "a_norm_g", "b_w_grp", "b_scale", "c_w_qkv", "c_w_out",
          "c_sink", "d_w_dq", "d_q_norm_g", "d_w_uq", "d_w_dkv", "d_kv_norm_g", "d_w_ukv", "d_w_out", "ffn_w_gu",
          "ffn_w_down", "ln_g", "ln_b"]
WSHAPES = {
    "meta_tokens": (16, 1024), "hg_lb_logits": (5, 1024), "a_w_in": (1, 1024, 5120), "a_w_out": (1, 1024, 1024),
    "a_norm_g": (1, 128), "b_w_grp": (1, 4, 256, 256), "b_scale": (1, 1024), "c_w_qkv": (1, 1024, 1536),
    "c_w_out": (1, 1024, 1024), "c_sink": (1, 8), "d_w_dq": (1, 1024, 256), "d_q_norm_g": (1, 256),
    "d_w_uq": (1, 256, 3072), "d_w_dkv": (1, 1024, 320), "d_kv_norm_g": (1, 256), "d_w_ukv": (1, 256, 4096),
    "d_w_out": (1, 2048, 1024), "ffn_w_gu": (4, 1024, 5632), "ffn_w_down": (4, 2816, 1024), "ln_g": (4, 2, 1024),
    "ln_b": (4, 2, 1024),
}


def build_program(Ts, plan):
    nc = bass.Bass("TRN2", target_bir_lowering=False)
    xs = [nc.dram_tensor(f"x{i}", [T - N_META, D], F32, kind="ExternalInput").ap() for i, T in enumerate(Ts)]
    outs = [nc.dram_tensor(f"y{i}", [T - N_META, D], F32, kind="ExternalOutput").ap() for i, T in enumerate(Ts)]
    W = {n: nc.dram_tensor(n, list(WSHAPES[n]), F32, kind="ExternalInput").ap() for n in WNAMES}
    consts = nc.dram_tensor("consts", [128, CONST_COLS], F32, kind="ExternalInput").ap()
    k = KB(nc)
    seqs = [Seq(nc, f"s{i}", T) for i, T in enumerate(Ts)]
    invcnt = [nc.dram_tensor(f"invcnt{i}", [4, T], F32, kind="ExternalInput").ap() for i, T in enumerate(Ts)]
    rope128 = [nc.dram_tensor(f"rope128_{i}", [2, 128, T], F32, kind="ExternalInput").ap() for i, T in enumerate(Ts)]
    rope64 = [nc.dram_tensor(f"rope64_{i}", [2, 64, T], F32, kind="ExternalInput").ap() for i, T in enumerate(Ts)]
    C = {}
    cc = k.bf(384)
    C["ident"] = cc[:, 0:128]
    C["mprev"] = cc[:, 128:256]
    C["mnext"] = cc[:, 256:384]
    C["dep"] = Dep()
    dsc = k.new_dsem("consts")
    k.dma("pool", cc, consts[:, 0:384], dsc, w=[C["dep"]])
    C["ones"] = k.bf(128)
    C["neghalf"] = k.f32(8)
    C["dep2"] = Dep()
    k.memset("pool", C["neghalf"], -0.5, w=[C["dep2"]])
    k.memset("pool", C["ones"], 1.0, w=[C["dep2"]])
    C["invcnt"] = invcnt
    C["rope128"] = rope128
    C["rope64"] = rope64
    C["keep"] = k.aoff
    k.dsem_base = k.dsem_idx
    prologue(k, C, seqs, xs, W["meta_tokens"])
    for name in plan:
        kind, li = name.split(":")
        li = int(li)
        if kind == "ffn":
            ffn_sublayer(k, C, seqs, W, li)
        elif kind == "mix" and li % 4 == 0:
            hgrn2_sublayer(k, C, seqs, W, li)
        elif kind == "mix" and li % 4 == 1:
            pool_sublayer(k, C, seqs, W, li)
        elif kind == "mix" and li % 4 == 2:
            swa_sublayer(k, C, seqs, W, li)
        elif kind == "mix" and li % 4 == 3:
            mla_sublayer(k, C, seqs, W, li)
        else:
            raise ValueError(name)
    k.barrier()
    epilogue_out(k, seqs, outs)
    k.barrier()
    k.emit()
    return nc, k


FULL_PLAN = ["mix:0", "ffn:0", "mix:1", "ffn:1", "mix:2", "ffn:2", "mix:3", "ffn:3"]


def run(inputs, Ts, plan, assign, n_cores=8, trace=False):
    nc, k = build_program(Ts, plan)
    consts = make_consts()
    in_maps = []
    for c in range(n_cores):
        m = {n: np.ascontiguousarray(inputs[n], dtype=np.float32) for n in WNAMES}
        m["consts"] = consts
        for i, T in enumerate(Ts):
            m[f"invcnt{i}"] = make_invcnt(T)
            m[f"rope128_{i}"] = make_rope_fm(T, 128)
            m[f"rope64_{i}"] = make_rope_fm(T, 64)
        for i, (nm, b) in enumerate(assign[c]):
            m[f"x{i}"] = np.ascontiguousarray(inputs[nm][b][:Ts[i] - N_META], dtype=np.float32)
        in_maps.append(m)
    res = run_bass_kernel_spmd(nc, in_maps, core_ids=list(range(n_cores)), trace=trace)
    return res, k


def kernel(**inputs):
    Ts = [4096 + N_META, 8192 + N_META]
    assign = [[("x_prompt", c), ("x_sample", c // 4)] for c in range(8)]
    res, _ = run(inputs, Ts, FULL_PLAN, assign)
    y_prompt = np.stack([res.results[c]["y0"] for c in range(8)], axis=0)
    y_sample = np.stack([res.results[0]["y1"], res.results[4]["y1"]], axis=0)
    return (y_prompt.astype(np.float32), y_sample.astype(np.float32))
```

```python
import math
import numpy as np
import concourse.bass as bass
import concourse.mybir as mybir
from concourse.bass_utils import run_bass_kernel_spmd

F32 = mybir.dt.float32
BF16 = mybir.dt.bfloat16
ALU = mybir.AluOpType
AF = mybir.ActivationFunctionType

D = 1024
DFF = 2816
N_META = 16
DEPTH = 4
DN_ALPHA = (2 * DEPTH) ** 0.25
LN_EPS = 1e-5
RMS_EPS = 1e-6
ARENA_WORDS = 52000


class Dep:
    __slots__ = ("w", "r")

    def __init__(self):
        self.w = {}
        self.r = {}


class KB:
    ENG = ("pe", "act", "dve", "pool", "sp")

    def __init__(self, nc):
        self.nc = nc
        self.stream = {e: [] for e in self.ENG}
        self.esem = {e: nc.alloc_semaphore("s_" + e) for e in ("pe", "act", "dve", "pool")}
        self.cnt = {e: 0 for e in self.esem}
        self.known = {e: {} for e in self.ENG}
        self.dsems = []
        self.dsem_idx = 0
        self.dsem_base = 0
        self.arena = nc.alloc_sbuf_tensor("arena", [128, ARENA_WORDS], F32)
        self.psum = nc.alloc_psum_tensor("psum", [128, 4096], F32)
        self.pbank = [Dep() for _ in range(8)]
        self.aoff = 0
        self.n_inst = 0

    def reset_arena(self, keep=0):
        self.aoff = keep

    def alloc(self, words):
        words = (words + 7) // 8 * 8
        o = self.aoff
        self.aoff += words
        assert self.aoff <= ARENA_WORDS, f"SBUF arena overflow {self.aoff}"
        return self.arena[:, o:o + words]

    def f32(self, n):
        return self.alloc(n)

    def bf(self, n):
        return self.alloc((n + 1) // 2).bitcast(BF16)[:, 0:n]

    def bank(self, i, n=512):
        return self.psum[:, i * 512:i * 512 + n]

    def new_dsem(self, name):
        if self.dsem_idx < len(self.dsems):
            s = self.dsems[self.dsem_idx]
        else:
            s = [self.nc.alloc_semaphore(f"d{len(self.dsems)}"), 0]
            self.dsems.append(s)
        self.dsem_idx += 1
        return s

    def sub_begin(self):
        self.barrier()
        self.dsem_idx = self.dsem_base

    def _collect(self, r, w, eng=None):
        d = {}
        for t in r:
            for num, tok in t.w.items():
                if num not in d or d[num][1] < tok[1]:
                    d[num] = tok
        own = self.esem[eng].num if eng in ("act", "dve") else None
        for t in w:
            for num, tok in t.w.items():
                if num not in d or d[num][1] < tok[1]:
                    d[num] = tok
            for num, tok in t.r.items():
                if num == own:
                    continue
                if num not in d or d[num][1] < tok[1]:
                    d[num] = tok
        return d

    def _wait(self, eng, deps):
        kn = self.known[eng]
        pe_num = self.esem["pe"].num
        for num, (sem, val) in deps.items():
            if eng == "pe" and num == pe_num:
                continue
            if kn.get(num, 0) < val:
                self.stream[eng].append(("w", sem, val))
                kn[num] = val
                self.n_inst += 1

    def _commit(self, tok, r, w):
        num = tok[0].num
        for t in r:
            t.r[num] = tok
        for t in w:
            t.w = {num: tok}
            t.r = {}

    def op(self, eng, fn, r=(), w=(), sig=True):
        self._wait(eng, self._collect(r, w, eng))
        sem = self.esem[eng]
        if sig:
            self.cnt[eng] += 1
            tok = (sem, self.cnt[eng])
            self.stream[eng].append(("i", fn, sem, 1))
        else:
            tok = (sem, self.cnt[eng] + 1)
            self.stream[eng].append(("i", fn, None, 0))
        self.n_inst += 1
        self._commit(tok, r, w)

    def dma(self, q, out, in_, ds, r=(), w=()):
        self._wait(q, self._collect(r, w))
        ds[1] += 16
        tok = (ds[0], ds[1])
        self.stream[q].append(("i", lambda e: e.dma_start(out=out, in_=in_), ds[0], 16))
        self.n_inst += 1
        self._commit(tok, r, w)

    def barrier(self):
        deps = {}
        for e, sem in self.esem.items():
            if self.cnt[e] > 0:
                deps[sem.num] = (sem, self.cnt[e])
        for s in self.dsems:
            if s[1] > 0:
                deps[s[0].num] = (s[0], s[1])
        for e in self.ENG:
            self._wait(e, deps)

    def mm(self, out, lhsT, rhs, start, stop, r=(), w=(), sig=False):
        self.op("pe", lambda e: e.matmul(out, lhsT, rhs, start=start, stop=stop), r=r, w=w, sig=sig)

    def tr(self, out, in_, ident, r=(), w=(), sig=False):
        self.op("pe", lambda e: e.transpose(out, in_, ident), r=r, w=w, sig=sig)


    def act(self, out, in_, func, bias=0.0, scale=1.0, accum_out=None, r=(), w=()):
        if accum_out is None:
            self.op("act", lambda e: e.activation(out=out, in_=in_, func=func, bias=bias, scale=scale), r=r, w=w)
        else:
            self.op("act", lambda e: e.activation(out=out, in_=in_, func=func, bias=bias, scale=scale,
                                                  accum_out=accum_out), r=r, w=w)

    def copy(self, eng, out, in_, r=(), w=()):
        if eng == "act":
            self.op("act", lambda e: e.copy(out=out, in_=in_), r=r, w=w)
        else:
            self.op(eng, lambda e: e.tensor_copy(out=out, in_=in_), r=r, w=w)

    def tt(self, eng, out, in0, in1, op, r=(), w=()):
        self.op(eng, lambda e: e.tensor_tensor(out=out, in0=in0, in1=in1, op=op), r=r, w=w)

    def ts(self, eng, out, in0, s1, s2, op0, op1=None, r=(), w=()):
        if op1 is None:
            self.op(eng, lambda e: e.tensor_scalar(out=out, in0=in0, scalar1=s1, scalar2=None, op0=op0), r=r, w=w)
        else:
            self.op(eng, lambda e: e.tensor_scalar(out=out, in0=in0, scalar1=s1, scalar2=s2, op0=op0, op1=op1), r=r, w=w)

    def stt(self, eng, out, in0, scalar, in1, op0, op1, r=(), w=()):
        self.op(eng, lambda e: e.scalar_tensor_tensor(out=out, in0=in0, scalar=scalar, in1=in1, op0=op0, op1=op1),
                r=r, w=w)

    def bn_stats(self, out, in_, r=(), w=()):
        self.op("dve", lambda e: e.bn_stats(out=out, in_=in_), r=r, w=w)

    def bn_aggr(self, out, in_, r=(), w=()):
        self.op("dve", lambda e: e.bn_aggr(out=out, in_=in_), r=r, w=w)

    def recip(self, out, in_, r=(), w=()):
        self.op("dve", lambda e: e.reciprocal(out=out, in_=in_), r=r, w=w)

    def memset(self, eng, out, val, r=(), w=()):
        self.op(eng, lambda e: e.memset(out, val), r=r, w=w)

    def emit(self):
        nc = self.nc
        streams = self.stream
        with nc.Block() as block:
            def run(e, lst):
                for ent in lst:
                    if ent[0] == "w":
                        e.wait_ge(ent[1], ent[2])
                    else:
                        ins = ent[1](e)
                        if ent[2] is not None:
                            ins.then_inc(ent[2], ent[3])

            @block.tensor
            def _(e):
                run(e, streams["pe"])

            @block.scalar
            def _(e):
                run(e, streams["act"])

            @block.vector
            def _(e):
                run(e, streams["dve"])

            @block.gpsimd
            def _(e):
                run(e, streams["pool"])

            @block.sync
            def _(e):
                run(e, streams["sp"])


class Seq:
    def __init__(self, nc, name, T):
        self.T = T
        self.name = name
        self.ngroups = (T + 511) // 512
        self.H = [nc.dram_tensor(f"{name}_H{i}", [T, D], F32, kind="Internal").ap() for i in range(2)]
        self.HT = [nc.dram_tensor(f"{name}_HT{i}", [D, T], BF16, kind="Internal").ap() for i in range(2)]
        self.dH = [[Dep() for _ in range(self.ngroups)] for _ in range(2)]
        self.dHT = [[Dep() for _ in range(self.ngroups)] for _ in range(2)]
        self.cur = 0

    def group(self, g):
        t0 = g * 512
        n = min(512, self.T - t0)
        return t0, n


def load_weight_bf16(k, dst3, src2, ds, wdep, kchunks):
    for c in range(kchunks):
        k.dma("pool", dst3[:, c, :], src2[c * 128:(c + 1) * 128, :], ds, w=[wdep])


def ln_epilogue(k, C, seq, dst, g, t0, nv, ti, y_ps, y_deps, hbuf, hdep, gam, bet, cdep, st, hT_out, hTo_dep,
                tr_bank):
    P = slice(0, nv)
    stats, mv, rstd, hb, sdep, hbdep, ds_out = st
    k.stt("dve", hbuf[P, :], hbuf[P, :], float(DN_ALPHA), y_ps[P, :], ALU.mult, ALU.add, r=list(y_deps), w=[hdep])
    ln_rows(k, C, seq, dst, g, t0, nv, ti, hbuf, hdep, gam, bet, cdep, st, hT_out, hTo_dep, tr_bank)


def ln_rows(k, C, seq, dst, g, t0, nv, ti, hbuf, hdep, gam, bet, cdep, st, hT_out, hTo_dep, tr_bank):
    P = slice(0, nv)
    stats, mv, rstd, hb, sdep, hbdep, ds_out = st
    for j in range(2):
        k.bn_stats(stats[P, 6 * j:6 * j + 6], hbuf[P, 512 * j:512 * j + 512], r=[hdep], w=[sdep])
    k.bn_aggr(mv[P, 0:2], stats[P, 0:12], r=[sdep], w=[sdep])
    k.ts("dve", rstd[P, 0:1], mv[P, 1:2], float(LN_EPS), None, ALU.add, r=[sdep], w=[sdep])
    k.tt("pool", rstd[P, 0:1], rstd[P, 0:1], C["neghalf"][P, 0:1], ALU.pow, r=[sdep, C["dep2"]], w=[sdep])
    k.ts("dve", hbuf[P, :], hbuf[P, :], mv[P, 0:1], rstd[P, 0:1], ALU.subtract, ALU.mult, r=[sdep], w=[hdep])
    k.tt("pool", hbuf[P, :], hbuf[P, :], gam[P, :], ALU.mult, r=[cdep], w=[hdep])
    k.tt("pool", hbuf[P, :], hbuf[P, :], bet[P, :], ALU.add, r=[cdep], w=[hdep])
    k.dma("sp", seq.H[dst][t0:t0 + nv, :], hbuf[P, :], ds_out, r=[hdep], w=[seq.dH[dst][g]])
    k.copy("act", hb[P, :], hbuf[P, :], r=[hdep], w=[hbdep])
    transpose_rows(k, C, hb, hbdep, nv, hT_out, hTo_dep, ti * 128, tr_bank)


def transpose_rows(k, C, hb, hbdep, nv, hT_out, hTo_dep, col0, tr_bank):
    pt = k.bank(tr_bank)
    for half in range(2):
        for j in range(4):
            c = half * 4 + j
            k.mm(pt[:, j * 128:j * 128 + nv], hb[0:nv, c * 128:(c + 1) * 128], C["ident"][0:nv, 0:nv], True, True,
                 r=[hbdep, C["dep"]], w=[k.pbank[tr_bank]], sig=(j == 3))
        k.copy("act", hT_out[:, half * 4:half * 4 + 4, col0:col0 + nv],
               pt.rearrange("p (c t) -> p c t", c=4)[:, :, 0:nv], r=[k.pbank[tr_bank]], w=[hTo_dep])


def ffn_sublayer(k, C, seqs, W, li):
    nc = k.nc
    k.sub_begin()
    k.reset_arena(C["keep"])
    wgu = k.bf(8 * 2 * DFF).rearrange("p (c n) -> p c n", c=8)
    wd = k.bf(22 * D).rearrange("p (c n) -> p c n", c=22)
    gam = k.f32(D)
    bet = k.f32(D)
    wdep, cdep = Dep(), Dep()
    dsw = k.new_dsem(f"ffw{li}")
    load_weight_bf16(k, wgu, W["ffn_w_gu"][li], dsw, wdep, 8)
    load_weight_bf16(k, wd, W["ffn_w_down"][li], dsw, wdep, 22)
    dsc = k.new_dsem(f"ffc{li}")
    k.dma("sp", gam, W["ln_g"][li, 1:2, :].partition_broadcast(128), dsc, w=[cdep])
    k.dma("sp", bet, W["ln_b"][li, 1:2, :].partition_broadcast(128), dsc, w=[cdep])
    hT_in = k.bf(8 * 512).rearrange("p (c n) -> p c n", c=8)
    aT = k.bf(22 * 512).rearrange("p (c n) -> p c n", c=22)
    sil = [k.f32(512) for _ in range(2)]
    hbufs = [k.f32(D) for _ in range(3)]
    hT_out = k.bf(8 * 512).rearrange("p (c n) -> p c n", c=8)
    stats = k.f32(16)
    mv = k.f32(8)
    rstd = k.f32(8)
    hb = k.bf(D)
    d_hTin, d_aT, d_hTo, d_s, d_hb = Dep(), Dep(), Dep(), Dep(), Dep()
    d_sil = [Dep(), Dep()]
    d_hbuf = [Dep() for _ in range(3)]
    ds_in = k.new_dsem(f"ffi{li}")
    ds_h = [k.new_dsem(f"ffh{li}_{i}") for i in range(3)]
    ds_o = k.new_dsem(f"ffo{li}")
    gu_banks = [0, 1, 2]
    y_banks = [(3, 4), (5, 6)]
    tr_bank = 7
    gi = 0
    yi = 0
    hi = 0
    for seq in seqs:
        src = seq.cur
        dst = 1 - src
        HTv = seq.HT[src].rearrange("(c p) t -> p c t", p=128)
        HTo = seq.HT[dst].rearrange("(c p) t -> p c t", p=128)
        for g in range(seq.ngroups):
            t0, n = seq.group(g)
            k.dma("sp", hT_in[:, :, 0:n], HTv[:, :, t0:t0 + n], ds_in, r=[seq.dHT[src][g]], w=[d_hTin])
            for c in range(22):
                bg = gu_banks[gi % 3]
                bu = gu_banks[(gi + 1) % 3]
                gi += 2
                for (b, col0) in ((bg, c * 128), (bu, DFF + c * 128)):
                    for kc in range(8):
                        k.mm(k.bank(b, n), wgu[:, kc, col0:col0 + 128], hT_in[:, kc, 0:n], kc == 0, kc == 7,
                             r=[wdep, d_hTin], w=[k.pbank[b]], sig=(kc == 7))
                s = c % 2
                k.act(sil[s][:, 0:n], k.bank(bg, n), AF.Silu, r=[k.pbank[bg]], w=[d_sil[s]])
                k.tt("dve", aT[:, c, 0:n], sil[s][:, 0:n], k.bank(bu, n), ALU.mult, r=[k.pbank[bu], d_sil[s]], w=[d_aT])
            ntile = (n + 127) // 128
            for ti in range(ntile):
                nv = min(128, n - ti * 128)
                tt0 = t0 + ti * 128
                hbuf = hbufs[hi % 3]
                hdep = d_hbuf[hi % 3]
                dsh = ds_h[hi % 3]
                hi += 1
                k.dma("sp", hbuf[0:nv, :], seq.H[src][tt0:tt0 + nv, :], dsh, r=[seq.dH[src][g]], w=[hdep])
                yb = y_banks[yi % 2]
                yi += 1
                y_ps = k.psum[:, yb[0] * 512:yb[0] * 512 + 1024]
                for half in range(2):
                    for c in range(22):
                        k.mm(k.bank(yb[half])[0:nv, :], aT[:, c, ti * 128:ti * 128 + nv], wd[:, c, half * 512:(half + 1) * 512],
                             c == 0, c == 21, r=[wdep, d_aT], w=[k.pbank[yb[half]]], sig=(c == 21))
                ln_epilogue(k, C, seq, dst, g, tt0, nv, ti, y_ps, [k.pbank[yb[0]], k.pbank[yb[1]]], hbuf, hdep, gam, bet,
                            cdep, (stats, mv, rstd, hb, d_s, d_hb, dsh), hT_out, d_hTo, tr_bank)
            k.dma("sp", HTo[:, :, t0:t0 + n], hT_out[:, :, 0:n], ds_o, r=[d_hTo], w=[seq.dHT[dst][g]])
        seq.cur = dst


def alloc_ln_state(k, li, which, W, tag):
    gam = k.f32(D)
    bet = k.f32(D)
    cdep = Dep()
    dsw = k.new_dsem(f"{tag}c{li}")
    k.dma("sp", gam, W["ln_g"][li, which:which + 1, :].partition_broadcast(128), dsw, w=[cdep])
    k.dma("sp", bet, W["ln_b"][li, which:which + 1, :].partition_broadcast(128), dsw, w=[cdep])
    stats = k.f32(16)
    mv = k.f32(8)
    rstd = k.f32(8)
    hb = k.bf(D)
    hT_out = k.bf(8 * 512).rearrange("p (c n) -> p c n", c=8)
    return dict(gam=gam, bet=bet, cdep=cdep, stats=stats, mv=mv, rstd=rstd, hb=hb, hT_out=hT_out, d_s=Dep(), d_hb=Dep(),
                d_hTo=Dep(), ds_o=k.new_dsem(f"{tag}o{li}"))


def pool_sublayer(k, C, seqs, W, li):
    k.sub_begin()
    k.reset_arena(C["keep"])
    L = alloc_ln_state(k, li, 0, W, "pl")
    wg = k.bf(4 * 2 * 256).rearrange("p (g c n) -> p g c n", g=4, c=2)
    wdep = Dep()
    dsw = k.new_dsem("plw")
    for g4 in range(4):
        for c in range(2):
            k.dma("pool", wg[:, g4, c, :], W["b_w_grp"][0, g4, c * 128:(c + 1) * 128, :], dsw, w=[wdep])
    scl = k.f32(D)
    sdep = Dep()
    k.dma("sp", scl, W["b_scale"][0:1, :].partition_broadcast(128), k.new_dsem("pls"), w=[sdep])
    PADW = 512 + 16
    xin = k.bf(8 * PADW).rearrange("p (c n) -> p c n", c=8)
    xf = k.f32(8 * PADW).rearrange("p (c n) -> p c n", c=8)
    sa = k.f32(2 * PADW).rearrange("p (c n) -> p c n", c=2)
    sb = k.f32(2 * PADW).rearrange("p (c n) -> p c n", c=2)
    icn = k.f32(4 * 512).rearrange("p (c n) -> p c n", c=4)
    pT = k.bf(8 * 512).rearrange("p (c n) -> p c n", c=8)
    ysc = k.f32(D)
    hbufs = [k.f32(D) for _ in range(2)]
    d_x, d_xf, d_sa, d_sb, d_ic, d_pT, d_y = Dep(), Dep(), Dep(), Dep(), Dep(), Dep(), Dep()
    d_hbuf = [Dep(), Dep()]
    ds_in = k.new_dsem("pli")
    ds_ic = k.new_dsem("plic")
    ds_h = [k.new_dsem(f"plh{i}") for i in range(2)]
    hi = 0
    yi = 0
    y_banks = [(0, 1), (2, 3)]
    for si, seq in enumerate(seqs):
        src = seq.cur
        dst = 1 - src
        T = seq.T
        HTv = seq.HT[src].rearrange("(c p) t -> p c t", p=128)
        HTo = seq.HT[dst].rearrange("(c p) t -> p c t", p=128)
        for g in range(seq.ngroups):
            t0, n = seq.group(g)
            lo = max(0, t0 - 8)
            hi_t = min(T, t0 + n + 8)
            off = lo - (t0 - 8)
            gdeps = [seq.dHT[src][gg] for gg in range(max(0, g - 1), min(seq.ngroups, g + 2))]
            k.memset("pool", xin[:, :, :], 0.0, w=[d_x])
            k.dma("sp", xin[:, :, off:off + (hi_t - lo)], HTv[:, :, lo:hi_t], ds_in, r=gdeps, w=[d_x])
            k.copy("dve", xf[:, :, :], xin[:, :, :], r=[d_x], w=[d_xf])
            k.dma("sp", icn[:, :, 0:n], C["invcnt"][si][:, t0:t0 + n].partition_broadcast(128), ds_ic, w=[d_ic])
            for g4 in range(4):
                cs = slice(2 * g4, 2 * g4 + 2)
                w = 2 << g4
                k.tt("dve", sa[:, :, 1:PADW], xf[:, cs, 0:PADW - 1], xf[:, cs, 1:PADW], ALU.add, r=[d_xf], w=[d_sa])
                cur, curd, oth, othd = sa, d_sa, sb, d_sb
                lo_v = 1
                hi_v = PADW
                sh = 1
                while (2 * sh) < w:
                    a0 = lo_v + sh
                    a1 = hi_v - sh
                    k.tt("dve", oth[:, :, a0:a1], cur[:, :, a0 - sh:a1 - sh], cur[:, :, a0 + sh:a1 + sh], ALU.add,
                         r=[curd], w=[othd])
                    cur, curd, oth, othd = oth, othd, cur, curd
                    lo_v, hi_v = a0, a1
                    sh *= 2
                assert lo_v <= 8 and hi_v >= 8 + n
                k.tt("dve", cur[:, :, 8:8 + n], cur[:, :, 8:8 + n],
                     icn[:, g4:g4 + 1, 0:n].to_broadcast([128, 2, n]), ALU.mult, r=[d_ic], w=[curd])
                k.tt("dve", pT[:, cs, 0:n], cur[:, :, 8:8 + n], xf[:, cs, 8:8 + n], ALU.subtract, r=[curd, d_xf], w=[d_pT])
            ntile = (n + 127) // 128
            for ti in range(ntile):
                nv = min(128, n - ti * 128)
                tt0 = t0 + ti * 128
                hbuf = hbufs[hi % 2]
                hdep = d_hbuf[hi % 2]
                dsh = ds_h[hi % 2]
                hi += 1
                k.dma("sp", hbuf[0:nv, :], seq.H[src][tt0:tt0 + nv, :], dsh, r=[seq.dH[src][g]], w=[hdep])
                yb = y_banks[yi % 2]
                yi += 1
                y_ps = k.psum[:, yb[0] * 512:yb[0] * 512 + 1024]
                for g4 in range(4):
                    b = yb[g4 // 2]
                    for c in range(2):
                        k.mm(k.bank(b)[0:nv, (g4 % 2) * 256:(g4 % 2) * 256 + 256], pT[:, 2 * g4 + c, ti * 128:ti * 128 + nv],
                             wg[:, g4, c, :], c == 0, c == 1, r=[wdep, d_pT], w=[k.pbank[b]], sig=(c == 1))
                k.tt("dve", ysc[0:nv, :], y_ps[0:nv, :], scl[0:nv, :], ALU.mult, r=[k.pbank[yb[0]], k.pbank[yb[1]], sdep],
                     w=[d_y])
                k.stt("dve", hbuf[0:nv, :], hbuf[0:nv, :], float(DN_ALPHA), ysc[0:nv, :], ALU.mult, ALU.add, r=[d_y],
                      w=[hdep])
                ln_rows(k, C, seq, dst, g, tt0, nv, ti, hbuf, hdep, L["gam"], L["bet"], L["cdep"],
                        (L["stats"], L["mv"], L["rstd"], L["hb"], L["d_s"], L["d_hb"], dsh), L["hT_out"], L["d_hTo"], 7)
            k.dma("sp", HTo[:, :, t0:t0 + n], L["hT_out"][:, :, 0:n], L["ds_o"], r=[L["d_hTo"]], w=[seq.dHT[dst][g]])
        seq.cur = dst


def make_invcnt(T):
    t = np.arange(T)
    out = np.zeros((4, T), np.float32)
    for gi, w in enumerate((2, 4, 8, 16)):
        lo = np.clip(t - w // 2, 0, T)
        hi = np.clip(t + w // 2, 0, T)
        out[gi] = 1.0 / (hi - lo).astype(np.float32)
    return out


def load_w(k, dst3, src2, ds, wdep, kchunks, c0=0, c1=None):
    for c in range(kchunks):
        k.dma("pool", dst3[:, c, :], src2[c * 128:(c + 1) * 128, c0:c1], ds, w=[wdep])


def rope_fm(k, out_bf, psa, psb, da, db, cos, sin, dtab, t1, t2, d1, d2, dout, P):
    k.tt("dve", t1, psa, cos, ALU.mult, r=[da, dtab], w=[d1])
    k.tt("dve", t2, psb, sin, ALU.mult, r=[db, dtab], w=[d2])
    k.tt("pool", out_bf, t1, t2, ALU.add, r=[d1, d2], w=[dout])


def swa_sublayer(k, C, seqs, W, li):
    nc = k.nc
    k.sub_begin()
    k.reset_arena(C["keep"])
    L = alloc_ln_state(k, li, 0, W, "sw")
    wq = k.bf(8 * 1536).rearrange("p (c n) -> p c n", c=8)
    wsw = k.bf(8 * 1280).rearrange("p (c n) -> p c n", c=8)
    wo = k.bf(8 * 1024).rearrange("p (c n) -> p c n", c=8)
    wdep = Dep()
    dsw = k.new_dsem("sww")
    wsrc = W["c_w_qkv"][0]
    load_w(k, wq, wsrc, dsw, wdep, 8)
    load_w(k, wo, W["c_w_out"][0], dsw, wdep, 8)
    wsw4 = wsw.rearrange("p c (h two d) -> p c h two d", two=2, d=64)
    for c in range(8):
        srcv = wsrc[c * 128:(c + 1) * 128, 0:1280].rearrange("p (h two d) -> p h two d", two=2, d=64)
        k.dma("pool", wsw4[:, c, :, 0, :], srcv[:, :, 1, :], dsw, w=[wdep])
        k.dma("pool", wsw4[:, c, :, 1, :], srcv[:, :, 0, :], dsw, w=[wdep])
    sk = k.f32(8)
    srow = k.bf(8 * 128).rearrange("p (h n) -> p h n", h=8)
    skdep = Dep()
    k.dma("sp", sk[0:1, 0:8], W["c_sink"][0:1, :], k.new_dsem("swsk"), w=[skdep])
    k.act(sk[0:1, 0:8], sk[0:1, 0:8], AF.Exp, r=[skdep], w=[skdep])
    k.copy("dve", srow[0:1, :, :], sk[0:1, 0:8].unsqueeze(2).to_broadcast([1, 8, 128]), r=[skdep], w=[skdep])
    scr = []
    for i, seq in enumerate(seqs):
        scr.append(dict(
            QT=nc.dram_tensor(f"swQT{i}", [1024, seq.T], BF16, kind="Internal").ap(),
            KT=nc.dram_tensor(f"swKT{i}", [256, seq.T], BF16, kind="Internal").ap(),
            V=nc.dram_tensor(f"swV{i}", [seq.T, 256], BF16, kind="Internal").ap()))
    keep2 = k.aoff
    hT_in = k.bf(8 * 512).rearrange("p (c n) -> p c n", c=8)
    tab = k.f32(2 * 512).rearrange("p (c n) -> p c n", c=2)
    t1 = k.f32(512)
    t2 = k.f32(512)
    qk = k.bf(10 * 512).rearrange("p (c n) -> p c n", c=10)
    vt = [k.bf(256) for _ in range(2)]
    d_in, d_tab, d1, d2, d_qk = Dep(), Dep(), Dep(), Dep(), Dep()
    d_vt = [Dep(), Dep()]
    ds_in = k.new_dsem("swi")
    ds_tab = k.new_dsem("swt")
    ds_o = k.new_dsem("swo")
    ds_vs = [k.new_dsem("swv0"), k.new_dsem("swv1")]
    bi = 0
    vi = 0
    for si, seq in enumerate(seqs):
        src = seq.cur
        HTv = seq.HT[src].rearrange("(c p) t -> p c t", p=128)
        QTo = scr[si]["QT"].rearrange("(c p) t -> p c t", p=128)
        KTo = scr[si]["KT"].rearrange("(c p) t -> p c t", p=128)
        for g in range(seq.ngroups):
            t0, n = seq.group(g)
            k.dma("sp", hT_in[:, :, 0:n], HTv[:, :, t0:t0 + n], ds_in, r=[seq.dHT[src][g]], w=[d_in])
            k.dma("sp", tab[:, :, 0:n], C["rope128"][si][:, :, t0:t0 + n].rearrange("c p t -> p c t"), ds_tab, w=[d_tab])
            for h in range(10):
                ba = bi % 4
                bb = (bi + 1) % 4
                bi += 2
                for (b, wt) in ((ba, wq), (bb, wsw)):
                    for kc in range(8):
                        k.mm(k.bank(b, n), wt[:, kc, h * 128:(h + 1) * 128], hT_in[:, kc, 0:n], kc == 0, kc == 7,
                             r=[wdep, d_in], w=[k.pbank[b]], sig=(kc == 7))
                rope_fm(k, qk[:, h, 0:n], k.bank(ba, n), k.bank(bb, n), k.pbank[ba], k.pbank[bb], tab[:, 0, 0:n],
                        tab[:, 1, 0:n], d_tab, t1[:, 0:n], t2[:, 0:n], d1, d2, d_qk, 128)
            k.dma("sp", QTo[:, :, t0:t0 + n], qk[:, 0:8, 0:n], ds_o, r=[d_qk])
            k.dma("sp", KTo[:, :, t0:t0 + n], qk[:, 8:10, 0:n], ds_o, r=[d_qk])
            for ti in range((n + 127) // 128):
                nv = min(128, n - ti * 128)
                tt0 = t0 + ti * 128
                for kc in range(8):
                    k.mm(k.bank(6)[0:nv, 0:256], hT_in[:, kc, ti * 128:ti * 128 + nv], wq[:, kc, 1280:1536], kc == 0, kc == 7,
                         r=[wdep, d_in], w=[k.pbank[6]], sig=(kc == 7))
                v = vt[vi % 2]
                dv = d_vt[vi % 2]
                dsv = ds_vs[vi % 2]
                vi += 1
                k.copy("act", v[0:nv, :], k.bank(6)[0:nv, 0:256], r=[k.pbank[6]], w=[dv])
                k.dma("sp", scr[si]["V"][tt0:tt0 + nv, :], v[0:nv, :], dsv, r=[dv])
    k.barrier()
    k.reset_arena(keep2)
    Qt = k.bf(8 * 128).rearrange("p (c n) -> p c n", c=8)
    Kw = k.bf(2 * 384).rearrange("p (c n) -> p c n", c=2)
    Vw = k.bf(3 * 256).rearrange("p (c n) -> p c n", c=3)
    Pt = [k.bf(128) for _ in range(2)]
    rden = k.f32(128)
    OT = k.bf(8 * 128).rearrange("p (c n) -> p c n", c=8)
    hbufs = [k.f32(D) for _ in range(2)]
    d_q, d_k, d_v, d_rd, d_OT = Dep(), Dep(), Dep(), Dep(), Dep()
    d_P = [Dep(), Dep()]
    d_hbuf = [Dep(), Dep()]
    ds_q = k.new_dsem("swq")
    ds_k = k.new_dsem("swk")
    ds_v = k.new_dsem("swv")
    ds_h = [k.new_dsem(f"swh{i}") for i in range(2)]
    SC = 128 ** -0.5
    pi = 0
    hi = 0
    for si, seq in enumerate(seqs):
        src = seq.cur
        dst = 1 - src
        T = seq.T
        NT = (T + 127) // 128
        QTo = scr[si]["QT"].rearrange("(c p) t -> p c t", p=128)
        KTo = scr[si]["KT"].rearrange("(c p) t -> p c t", p=128)
        HTo = seq.HT[dst].rearrange("(c p) t -> p c t", p=128)
        for i in range(NT):
            q0 = i * 128
            nvq = min(128, T - q0)
            g = i // 4
            ti = i % 4
            t0g, ng = seq.group(g)
            js = [j for j in (i - 1, i, i + 1) if 0 <= j < NT]
            klo = js[0] * 128
            khi = min(T, js[-1] * 128 + 128)
            k.dma("sp", Qt[:, :, 0:nvq], QTo[:, :, q0:q0 + nvq], ds_q, w=[d_q])
            k.dma("sp", Kw[:, :, 0:khi - klo], KTo[:, :, klo:khi], ds_k, w=[d_k])
            for jj, j in enumerate(js):
                nk = min(128, T - j * 128)
                k.dma("sp", Vw[0:nk, jj, :], scr[si]["V"][j * 128:j * 128 + nk, :], ds_v, w=[d_v])
            hbuf = hbufs[hi % 2]
            hdep = d_hbuf[hi % 2]
            dsh = ds_h[hi % 2]
            hi += 1
            k.dma("sp", hbuf[0:nvq, :], seq.H[src][q0:q0 + nvq, :], dsh, r=[seq.dH[src][g]], w=[hdep])
            for h in range(8):
                gq = h // 4
                for jj, j in enumerate(js):
                    nk = min(128, T - j * 128)
                    sbk = pi % 2
                    P = Pt[pi % 2]
                    dP = d_P[pi % 2]
                    pi += 1
                    k.mm(k.bank(sbk)[0:nk, 0:nvq], Kw[:, gq, jj * 128:jj * 128 + nk], Qt[:, h, 0:nvq], True, True,
                         r=[d_k, d_q], w=[k.pbank[sbk]], sig=True)
                    k.act(P[0:nk, 0:nvq], k.bank(sbk)[0:nk, 0:nvq], AF.Exp, scale=SC, r=[k.pbank[sbk]], w=[dP])
                    if j != i:
                        m = C["mprev"] if j < i else C["mnext"]
                        k.tt("pool", P[0:nk, 0:nvq], P[0:nk, 0:nvq], m[0:nk, 0:nvq], ALU.mult, r=[C["dep"]], w=[dP])
                    k.mm(k.bank(2)[:, 0:nvq], Vw[0:nk, jj, gq * 128:(gq + 1) * 128], P[0:nk, 0:nvq], jj == 0,
                         jj == len(js) - 1, r=[d_v, dP], w=[k.pbank[2]], sig=(jj == len(js) - 1))
                    k.mm(k.bank(3)[:, 0:nvq], C["ones"][0:nk, :], P[0:nk, 0:nvq], jj == 0, False,
                         r=[dP, C["dep2"]], w=[k.pbank[3]], sig=False)
                k.mm(k.bank(3)[:, 0:nvq], srow[0:1, h, :], C["ones"][0:1, 0:nvq], False, True, r=[skdep, C["dep2"]],
                     w=[k.pbank[3]], sig=True)
                k.recip(rden[:, 0:nvq], k.bank(3)[:, 0:nvq], r=[k.pbank[3]], w=[d_rd])
                k.tt("dve", OT[:, h, 0:nvq], k.bank(2)[:, 0:nvq], rden[:, 0:nvq], ALU.mult, r=[k.pbank[2], d_rd], w=[d_OT])
            y_ps = k.psum[:, 4 * 512:4 * 512 + 1024]
            for half in range(2):
                for h in range(8):
                    k.mm(k.bank(4 + half)[0:nvq, :], OT[:, h, 0:nvq], wo[:, h, half * 512:(half + 1) * 512], h == 0, h == 7,
                         r=[wdep, d_OT], w=[k.pbank[4 + half]], sig=(h == 7))
            ln_epilogue(k, C, seq, dst, g, q0, nvq, ti, y_ps, [k.pbank[4], k.pbank[5]], hbuf, hdep, L["gam"], L["bet"],
                        L["cdep"], (L["stats"], L["mv"], L["rstd"], L["hb"], L["d_s"], L["d_hb"], dsh), L["hT_out"],
                        L["d_hTo"], 7)
            if ti == 3 or i == NT - 1:
                k.dma("sp", HTo[:, :, t0g:t0g + ng], L["hT_out"][:, :, 0:ng], L["ds_o"], r=[L["d_hTo"]],
                      w=[seq.dHT[dst][g]])
        seq.cur = dst


def make_rope_fm(T, d):
    inv = 1.0 / (10000.0 ** (np.arange(0, d, 2, dtype=np.float32) / d))
    ang = np.arange(T, dtype=np.float32)[None, :] * inv[:, None].astype(np.float32)
    cos = np.cos(ang).astype(np.float32)
    sin = np.sin(ang).astype(np.float32)
    out = np.zeros((2, d, T), np.float32)
    out[0, :d // 2] = cos
    out[0, d // 2:] = cos
    out[1, :d // 2] = -sin
    out[1, d // 2:] = sin
    return out


def mla_sublayer(k, C, seqs, W, li):
    nc = k.nc
    k.sub_begin()
    k.reset_arena(C["keep"])
    L = alloc_ln_state(k, li, 0, W, "ml")
    Tmax = max(seq.T for seq in seqs)
    cqn = k.bf(2 * Tmax).rearrange("p (c n) -> p c n", c=2)
    cn = k.bf(2 * Tmax).rearrange("p (c n) -> p c n", c=2)
    krT = k.bf(Tmax)
    gq = k.f32(8)
    gkv = k.f32(8)
    gdep = Dep()
    dsg = k.new_dsem("mlg")
    for c in range(2):
        k.dma("sp", gq[:, c:c + 1], W["d_q_norm_g"][0:1, c * 128:(c + 1) * 128].rearrange("o p -> p o"), dsg, w=[gdep])
        k.dma("sp", gkv[:, c:c + 1], W["d_kv_norm_g"][0:1, c * 128:(c + 1) * 128].rearrange("o p -> p o"), dsg, w=[gdep])
    d_cqn, d_cn, d_kr = Dep(), Dep(), Dep()
    keep2 = k.aoff
    SC = 192 ** -0.5
    for si, seq in enumerate(seqs):
        T = seq.T
        NT = (T + 127) // 128
        src = seq.cur
        dst = 1 - src
        HTv = seq.HT[src].rearrange("(c p) t -> p c t", p=128)
        HTo = seq.HT[dst].rearrange("(c p) t -> p c t", p=128)
        OTs = nc.dram_tensor(f"mlOT{si}", [2048, T], BF16, kind="Internal").ap()
        OTv = OTs.rearrange("(c p) t -> p c t", p=128)
        k.barrier()
        k.reset_arena(keep2)
        wdq = k.bf(8 * 256).rearrange("p (c n) -> p c n", c=8)
        wdkv = k.bf(8 * 320).rearrange("p (c n) -> p c n", c=8)
        wkrs = k.bf(8 * 64).rearrange("p (c n) -> p c n", c=8)
        wdep = Dep()
        dsw = k.new_dsem(f"mlw1_{si}")
        load_w(k, wdq, W["d_w_dq"][0], dsw, wdep, 8)
        load_w(k, wdkv, W["d_w_dkv"][0], dsw, wdep, 8)
        for c in range(8):
            k.dma("pool", wkrs[:, c, 0:32], W["d_w_dkv"][0, c * 128:(c + 1) * 128, 288:320], dsw, w=[wdep])
            k.dma("pool", wkrs[:, c, 32:64], W["d_w_dkv"][0, c * 128:(c + 1) * 128, 256:288], dsw, w=[wdep])
        hT_in = k.bf(8 * 512).rearrange("p (c n) -> p c n", c=8)
        sq = k.bf(2 * 512).rearrange("p (c n) -> p c n", c=2)
        rs = k.f32(512)
        tab = k.f32(2 * 512).rearrange("p (c n) -> p c n", c=2)
        t1 = k.f32(512)
        t2 = k.f32(512)
        d_in, d_sq, d_rs, d_tab, d1, d2 = Dep(), Dep(), Dep(), Dep(), Dep(), Dep()
        ds_in = k.new_dsem(f"mli_{si}")
        ds_tab = k.new_dsem(f"mlt1_{si}")
        for g in range(seq.ngroups):
            t0, n = seq.group(g)
            k.dma("sp", hT_in[:, :, 0:n], HTv[:, :, t0:t0 + n], ds_in, r=[seq.dHT[src][g]], w=[d_in])
            k.dma("sp", tab[0:64, :, 0:n], C["rope64"][si][:, :, t0:t0 + n].rearrange("c p t -> p c t"), ds_tab, w=[d_tab])
            for (wt, gcol, outb, dout) in ((wdq, gq, cqn, d_cqn), (wdkv, gkv, cn, d_cn)):
                for c in range(2):
                    for kc in range(8):
                        k.mm(k.bank(c, n), wt[:, kc, c * 128:(c + 1) * 128], hT_in[:, kc, 0:n], kc == 0, kc == 7,
                             r=[wdep, d_in], w=[k.pbank[c]], sig=(kc == 7))
                    k.act(sq[:, c, 0:n], k.bank(c, n), AF.Square, r=[k.pbank[c]], w=[d_sq])
                for c in range(2):
                    k.mm(k.bank(2, n), C["ones"][:, :], sq[:, c, 0:n], c == 0, c == 1, r=[d_sq, C["dep2"]], w=[k.pbank[2]],
                         sig=(c == 1))
                k.ts("dve", rs[:, 0:n], k.bank(2, n), 1.0 / 256.0, float(RMS_EPS), ALU.mult, ALU.add, r=[k.pbank[2]], w=[d_rs])
                k.tt("pool", rs[:, 0:n], rs[:, 0:n], C["neghalf"][:, 0:1].to_broadcast([128, n]), ALU.pow, r=[C["dep2"]],
                     w=[d_rs])
                for c in range(2):
                    k.stt("dve", outb[:, c, t0:t0 + n], k.bank(c, n), gcol[:, c:c + 1], rs[:, 0:n], ALU.mult, ALU.mult,
                          r=[k.pbank[c], d_rs, gdep], w=[dout])
            for (b, wt, c0) in ((4, wdkv, 256), (5, wkrs, 0)):
                for kc in range(8):
                    k.mm(k.bank(b)[0:64, 0:n], wt[:, kc, c0:c0 + 64], hT_in[:, kc, 0:n], kc == 0, kc == 7, r=[wdep, d_in],
                         w=[k.pbank[b]], sig=(kc == 7))
            rope_fm(k, krT[0:64, t0:t0 + n], k.bank(4)[0:64, 0:n], k.bank(5)[0:64, 0:n], k.pbank[4], k.pbank[5],
                    tab[0:64, 0, 0:n], tab[0:64, 1, 0:n], d_tab, t1[0:64, 0:n], t2[0:64, 0:n], d1, d2, d_kr, 64)
        k.barrier()
        k.reset_arena(keep2)
        wuq = k.bf(2 * 3072).rearrange("p (c n) -> p c n", c=2)
        wuqs = k.bf(2 * 1024).rearrange("p (c n) -> p c n", c=2)
        wukv = k.bf(2 * 4096).rearrange("p (c n) -> p c n", c=2)
        wdep = Dep()
        dsw = k.new_dsem(f"mlw2_{si}")
        load_w(k, wuq, W["d_w_uq"][0], dsw, wdep, 2)
        load_w(k, wukv, W["d_w_ukv"][0], dsw, wdep, 2)
        wuqs4 = wuqs.rearrange("p c (h two d) -> p c h two d", two=2, d=32)
        for c in range(2):
            srcv = W["d_w_uq"][0, c * 128:(c + 1) * 128, :].rearrange("p (h e) -> p h e", e=192)
            k.dma("pool", wuqs4[:, c, :, 0, :], srcv[:, :, 160:192], dsw, w=[wdep])
            k.dma("pool", wuqs4[:, c, :, 1, :], srcv[:, :, 128:160], dsw, w=[wdep])
        KnT = k.bf(Tmax)
        Vh = k.bf(NT * 128).rearrange("p (j d) -> p j d", d=128)
        QnT = k.bf(512)
        QrT = k.bf(512)
        Pt = [k.bf(512) for _ in range(2)]
        rden = k.f32(512)
        OTg = k.bf(512)
        tab = k.f32(2 * 512).rearrange("p (c n) -> p c n", c=2)
        t1 = k.f32(512)
        t2 = k.f32(512)
        d_Kn, d_Vh, d_Qn, d_Qr, d_rd, d_OTg, d_tab, d1, d2 = (Dep() for _ in range(9))
        d_P = [Dep(), Dep()]
        ds_t = k.new_dsem(f"mlt_{si}")
        ds_ot = k.new_dsem(f"mlot_{si}")
        pi = 0
        kb = 0
        for h in range(16):
            for g in range(seq.ngroups):
                t0, n = seq.group(g)
                b = 6 + (kb % 2)
                kb += 1
                for kc in range(2):
                    k.mm(k.bank(b, n), wukv[:, kc, h * 256:h * 256 + 128], cn[:, kc, t0:t0 + n], kc == 0, kc == 1,
                         r=[wdep, d_cn], w=[k.pbank[b]], sig=(kc == 1))
                k.copy("act", KnT[:, t0:t0 + n], k.bank(b, n), r=[k.pbank[b]], w=[d_Kn])
            for j in range(NT):
                nk = min(128, T - j * 128)
                b = 6 + (kb % 2)
                kb += 1
                for kc in range(2):
                    k.mm(k.bank(b)[0:nk, 0:128], cn[:, kc, j * 128:j * 128 + nk], wukv[:, kc, h * 256 + 128:h * 256 + 256],
                         kc == 0, kc == 1, r=[wdep, d_cn], w=[k.pbank[b]], sig=(kc == 1))
                k.copy("dve", Vh[0:nk, j, :], k.bank(b)[0:nk, 0:128], r=[k.pbank[b]], w=[d_Vh])
            for g in range(seq.ngroups):
                t0, n = seq.group(g)
                k.dma("sp", tab[0:64, :, 0:n], C["rope64"][si][:, :, t0:t0 + n].rearrange("c p t -> p c t"), ds_t, w=[d_tab])
                for kc in range(2):
                    k.mm(k.bank(4, n), wuq[:, kc, h * 192:h * 192 + 128], cqn[:, kc, t0:t0 + n], kc == 0, kc == 1,
                         r=[wdep, d_cqn], w=[k.pbank[4]], sig=(kc == 1))
                k.copy("act", QnT[:, 0:n], k.bank(4, n), r=[k.pbank[4]], w=[d_Qn])
                for (b, wt, c0) in ((5, wuq, h * 192 + 128), (6 + (kb % 2), wuqs, h * 64)):
                    for kc in range(2):
                        k.mm(k.bank(b)[0:64, 0:n], wt[:, kc, c0:c0 + 64], cqn[:, kc, t0:t0 + n], kc == 0, kc == 1,
                             r=[wdep, d_cqn], w=[k.pbank[b]], sig=(kc == 1))
                bsw = 6 + (kb % 2)
                kb += 1
                rope_fm(k, QrT[0:64, 0:n], k.bank(5)[0:64, 0:n], k.bank(bsw)[0:64, 0:n], k.pbank[5], k.pbank[bsw],
                        tab[0:64, 0, 0:n], tab[0:64, 1, 0:n], d_tab, t1[0:64, 0:n], t2[0:64, 0:n], d1, d2, d_Qr, 64)
                for j in range(NT):
                    nk = min(128, T - j * 128)
                    sbk = pi % 2
                    P = Pt[pi % 2]
                    dP = d_P[pi % 2]
                    pi += 1
                    k.mm(k.bank(sbk)[0:nk, 0:n], KnT[:, j * 128:j * 128 + nk], QnT[:, 0:n], True, False,
                         r=[d_Kn, d_Qn], w=[k.pbank[sbk]], sig=False)
                    k.mm(k.bank(sbk)[0:nk, 0:n], krT[0:64, j * 128:j * 128 + nk], QrT[0:64, 0:n], False, True,
                         r=[d_kr, d_Qr], w=[k.pbank[sbk]], sig=True)
                    k.act(P[0:nk, 0:n], k.bank(sbk)[0:nk, 0:n], AF.Exp, scale=SC, r=[k.pbank[sbk]], w=[dP])
                    k.mm(k.bank(2, n), Vh[0:nk, j, :], P[0:nk, 0:n], j == 0, j == NT - 1, r=[d_Vh, dP], w=[k.pbank[2]],
                         sig=(j == NT - 1))
                    k.mm(k.bank(3, n), C["ones"][0:nk, :], P[0:nk, 0:n], j == 0, j == NT - 1, r=[dP, C["dep2"]],
                         w=[k.pbank[3]], sig=(j == NT - 1))
                k.recip(rden[:, 0:n], k.bank(3, n), r=[k.pbank[3]], w=[d_rd])
                k.tt("dve", OTg[:, 0:n], k.bank(2, n), rden[:, 0:n], ALU.mult, r=[k.pbank[2], d_rd], w=[d_OTg])
                k.dma("sp", OTs[h * 128:(h + 1) * 128, t0:t0 + n], OTg[:, 0:n], ds_ot, r=[d_OTg])
        k.barrier()
        k.reset_arena(keep2)
        wout = k.bf(16 * 1024).rearrange("p (c n) -> p c n", c=16)
        wdep = Dep()
        dsw = k.new_dsem(f"mlw3_{si}")
        load_w(k, wout, W["d_w_out"][0], dsw, wdep, 16)
        OT = k.bf(16 * 512).rearrange("p (c n) -> p c n", c=16)
        hbufs = [k.f32(D) for _ in range(2)]
        d_OT = Dep()
        d_hbuf = [Dep(), Dep()]
        ds_q = k.new_dsem(f"mlq_{si}")
        ds_h = [k.new_dsem(f"mlh{i}_{si}") for i in range(2)]
        hi = 0
        yi = 0
        y_banks = [(0, 1), (2, 3)]
        for g in range(seq.ngroups):
            t0, n = seq.group(g)
            k.dma("sp", OT[:, :, 0:n], OTv[:, :, t0:t0 + n], ds_q, w=[d_OT])
            for ti in range((n + 127) // 128):
                nv = min(128, n - ti * 128)
                tt0 = t0 + ti * 128
                hbuf = hbufs[hi % 2]
                hdep = d_hbuf[hi % 2]
                dsh = ds_h[hi % 2]
                hi += 1
                k.dma("sp", hbuf[0:nv, :], seq.H[src][tt0:tt0 + nv, :], dsh, r=[seq.dH[src][g]], w=[hdep])
                yb = y_banks[yi % 2]
                yi += 1
                y_ps = k.psum[:, yb[0] * 512:yb[0] * 512 + 1024]
                for half in range(2):
                    for h in range(16):
                        k.mm(k.bank(yb[half])[0:nv, :], OT[:, h, ti * 128:ti * 128 + nv], wout[:, h, half * 512:(half + 1) * 512],
                             h == 0, h == 15, r=[wdep, d_OT], w=[k.pbank[yb[half]]], sig=(h == 15))
                ln_epilogue(k, C, seq, dst, g, tt0, nv, ti, y_ps, [k.pbank[yb[0]], k.pbank[yb[1]]], hbuf, hdep, L["gam"],
                            L["bet"], L["cdep"], (L["stats"], L["mv"], L["rstd"], L["hb"], L["d_s"], L["d_hb"], dsh),
                            L["hT_out"], L["d_hTo"], 7)
            k.dma("sp", HTo[:, :, t0:t0 + n], L["hT_out"][:, :, 0:n], L["ds_o"], r=[L["d_hTo"]], w=[seq.dHT[dst][g]])
        seq.cur = dst


def hgrn2_sublayer(k, C, seqs, W, li):
    nc = k.nc
    k.sub_begin()
    k.reset_arena(C["keep"])
    CH = 32
    NB = 256
    L = alloc_ln_state(k, li, 0, W, "hg")
    lg = k.f32(40).rearrange("p (r h) -> p r h", r=5)
    LB = k.f32(8)
    OML = k.f32(8)
    ssum = k.f32(8)
    ng = k.f32(128)
    dlb = Dep()
    dsl = k.new_dsem("hgl")
    for r in range(5):
        for h in range(8):
            k.dma("sp", lg[:, r, h:h + 1], W["hg_lb_logits"][r:r + 1, h * 128:(h + 1) * 128].rearrange("o p -> p o"), dsl,
                  w=[dlb])
    k.dma("sp", ng, W["a_norm_g"][0:1, :].partition_broadcast(128), dsl, w=[dlb])
    k.act(lg[:, :, :], lg[:, :, :], AF.Exp, r=[dlb], w=[dlb])
    k.tt("dve", ssum[:, 0:8], lg[:, 0, :], lg[:, 1, :], ALU.add, r=[dlb], w=[dlb])
    for r in range(2, li + 1):
        pass
    for r in range(2, 5):
        k.tt("dve", ssum[:, 0:8], ssum[:, 0:8], lg[:, r, :], ALU.add, r=[dlb], w=[dlb])
    k.recip(ssum[:, 0:8], ssum[:, 0:8], r=[dlb], w=[dlb])
    k.tt("dve", LB[:, 0:8], lg[:, 0, :], ssum[:, 0:8], ALU.mult, r=[dlb], w=[dlb])
    k.ts("dve", OML[:, 0:8], LB[:, 0:8], -1.0, 1.0, ALU.mult, ALU.add, r=[dlb], w=[dlb])
    win = k.bf(8 * 4096).rearrange("p (c n) -> p c n", c=8)
    wout = k.bf(8 * 1024).rearrange("p (c n) -> p c n", c=8)
    wdep = Dep()
    hT_in = k.bf(8 * NB).rearrange("p (c n) -> p c n", c=8)
    B1 = k.f32(8 * NB).rearrange("p (h n) -> p h n", h=8)
    B2 = k.f32(8 * NB).rearrange("p (h n) -> p h n", h=8)
    B3 = k.f32(8 * NB).rearrange("p (h n) -> p h n", h=8)
    B4 = k.f32(8 * NB).rearrange("p (h n) -> p h n", h=8)
    qe = k.bf(8 * NB).rearrange("p (h n) -> p h n", h=8)
    ke = k.bf(8 * NB).rearrange("p (h n) -> p h n", h=8)
    kd = k.bf(8 * NB).rearrange("p (h n) -> p h n", h=8)
    S = k.f32(1024).rearrange("p (h e) -> p h e", h=8)
    Sb = k.bf(1024).rearrange("p (h e) -> p h e", h=8)
    Vc = k.bf(1024)
    kdt = k.bf(1024)
    At = k.bf(256).rearrange("p (h t) -> p h t", h=8)
    osb = [k.f32(1024) for _ in range(2)]
    ofb = k.f32(1024)
    sqb = k.f32(1024)
    sgl = k.f32(1024)
    ss = k.f32(8)
    ob = k.bf(1024)
    oT = k.bf(8 * 512).rearrange("p (c n) -> p c n", c=8)
    hbufs = [k.f32(D) for _ in range(2)]
    d_in, d1, d2, d3, d4, d_qe, d_ke, d_kd, d_S, d_Sb, d_Vc, d_kdt, d_At = (Dep() for _ in range(13))
    d_osb = [Dep(), Dep()]
    d_of, d_sq, d_sgl, d_ss, d_ob, d_oT = (Dep() for _ in range(6))
    d_hbuf = [Dep(), Dep()]
    ds_w = k.new_dsem("hgw")
    ds_in = k.new_dsem("hgi")
    ds_os = [k.new_dsem("hgo0"), k.new_dsem("hgo1")]
    ds_of = k.new_dsem("hgof")
    ds_h = [k.new_dsem("hgh0"), k.new_dsem("hgh1")]
    OF = [nc.dram_tensor(f"hgOF{i}", [seq.T, D], F32, kind="Internal").ap() for i, seq in enumerate(seqs)]
    d_OF = [Dep() for _ in seqs]
    cnt = dict(pb=0, os=0, hi=0)

    def proj_fm(col0, n):
        b = cnt["pb"] % 2
        cnt["pb"] += 1
        for kc in range(8):
            k.mm(k.bank(b, n), win[:, kc, col0:col0 + 128], hT_in[:, kc, 0:n], kc == 0, kc == 7, r=[wdep, d_in],
                 w=[k.pbank[b]], sig=(kc == 7))
        return b

    def proj_tm(col0, c0, cs, banks):
        for half in range(2):
            for kc in range(8):
                k.mm(k.bank(banks[half])[0:cs, :], hT_in[:, kc, c0:c0 + cs], win[:, kc, col0 + half * 512:col0 + (half + 1) * 512],
                     kc == 0, kc == 7, r=[wdep, d_in], w=[k.pbank[banks[half]]], sig=(kc == 7))

    for direction in ("fwd", "bwd"):
        fwd = direction == "fwd"
        k.barrier()
        load_w(k, win[:, :, 0:2048], W["a_w_in"][0], ds_w, wdep, 8, 0, 2048)
        if fwd:
            load_w(k, win[:, :, 2048:3072], W["a_w_in"][0], ds_w, wdep, 8, 2048, 3072)
        else:
            load_w(k, win[:, :, 2048:4096], W["a_w_in"][0], ds_w, wdep, 8, 3072, 5120)
            load_w(k, wout, W["a_w_out"][0], ds_w, wdep, 8)
        mask = C["mnext"] if fwd else C["mprev"]
        for si, seq in enumerate(seqs):
            T = seq.T
            src = seq.cur
            dst = 1 - src
            HTv = seq.HT[src].rearrange("(c p) t -> p c t", p=128)
            HTo = seq.HT[dst].rearrange("(c p) t -> p c t", p=128)
            k.memset("dve", S[:, :, :], 0.0, w=[d_S])
            k.memset("pool", Sb[:, :, :], 0.0, w=[d_Sb])
            groups = list(range(seq.ngroups))
            if not fwd:
                groups = groups[::-1]
            for g in groups:
                tg0, ng_ = seq.group(g)
                halves = [(tg0 + o, min(NB, ng_ - o)) for o in range(0, ng_, NB)]
                if not fwd:
                    halves = halves[::-1]
                for (t0, n) in halves:
                    ch = min(CH, n)
                    nch = n // ch
                    k.dma("sp", hT_in[:, :, 0:n], HTv[:, :, t0:t0 + n], ds_in, r=[seq.dHT[src][g]], w=[d_in])
                    for h in range(8):
                        b = proj_fm(h * 128, n)
                        k.act(B4[:, h, 0:n], k.bank(b, n), AF.Silu, r=[k.pbank[b]], w=[d4])
                    for h in range(8):
                        b = proj_fm(2048 + h * 128, n)
                        k.act(B1[:, h, 0:n], k.bank(b, n), AF.Sigmoid, r=[k.pbank[b]], w=[d1])
                    k.tt("dve", B1[:, :, 0:n], B1[:, :, 0:n], OML[:, 0:8].unsqueeze(2).to_broadcast([128, 8, n]), ALU.mult,
                         r=[dlb], w=[d1])
                    k.tt("dve", B1[:, :, 0:n], B1[:, :, 0:n], LB[:, 0:8].unsqueeze(2).to_broadcast([128, 8, n]), ALU.add,
                         r=[dlb], w=[d1])
                    k.act(B2[:, :, 0:n], B1[:, :, 0:n], AF.Ln, r=[d1], w=[d2])
                    k.ts("dve", B1[:, :, 0:n], B1[:, :, 0:n], -1.0, 1.0, ALU.mult, ALU.add, w=[d1])
                    cur, dcur, oth, doth = B2, d2, B3, d3
                    sh = 1
                    while sh < ch:
                        c4 = cur[:, :, 0:n].rearrange("p h (c t) -> p h c t", t=ch)
                        o4 = oth[:, :, 0:n].rearrange("p h (c t) -> p h c t", t=ch)
                        for h in range(8):
                            if fwd:
                                k.tt("dve", o4[:, h, :, sh:ch], c4[:, h, :, sh:ch], c4[:, h, :, 0:ch - sh], ALU.add, r=[dcur],
                                     w=[doth])
                            else:
                                k.tt("dve", o4[:, h, :, 0:ch - sh], c4[:, h, :, 0:ch - sh], c4[:, h, :, sh:ch], ALU.add,
                                     r=[dcur], w=[doth])
                        for h in range(8):
                            if fwd:
                                k.copy("pool", o4[:, h, :, 0:sh], c4[:, h, :, 0:sh], r=[dcur], w=[doth])
                            else:
                                k.copy("pool", o4[:, h, :, ch - sh:ch], c4[:, h, :, ch - sh:ch], r=[dcur], w=[doth])
                        cur, dcur, oth, doth = oth, doth, cur, dcur
                        sh *= 2
                    bb, dbb, eb, deb = cur, dcur, oth, doth
                    Lidx = ch - 1 if fwd else 0
                    k.act(eb[:, :, 0:n], bb[:, :, 0:n], AF.Exp, r=[dbb], w=[deb])
                    k.tt("dve", qe[:, :, 0:n], B4[:, :, 0:n], eb[:, :, 0:n], ALU.mult, r=[d4, deb], w=[d_qe])
                    k.act(B4[:, :, 0:n], bb[:, :, 0:n], AF.Exp, scale=-1.0, r=[dbb], w=[d4])
                    k.tt("dve", ke[:, :, 0:n], B1[:, :, 0:n], B4[:, :, 0:n], ALU.mult, r=[d1, d4], w=[d_ke])
                    b4 = bb[:, :, 0:n].rearrange("p h (c t) -> p h c t", t=ch)
                    t4 = B4[:, :, 0:n].rearrange("p h (c t) -> p h c t", t=ch)
                    for h in range(8):
                        k.tt("dve", t4[:, h, :, :], b4[:, h, :, Lidx:Lidx + 1].to_broadcast([128, nch, ch]), b4[:, h, :, :],
                             ALU.subtract, r=[dbb], w=[d4])
                    k.act(B4[:, :, 0:n], B4[:, :, 0:n], AF.Exp, w=[d4])
                    k.tt("dve", kd[:, :, 0:n], B1[:, :, 0:n], B4[:, :, 0:n], ALU.mult, r=[d1, d4], w=[d_kd])
                    e4 = eb[:, :, 0:n].rearrange("p h (c t) -> p h c t", t=ch)
                    chunks = list(range(nch))
                    if not fwd:
                        chunks = chunks[::-1]
                    for c in chunks:
                        c0 = c * ch
                        tc0 = t0 + c0
                        cs = ch
                        proj_tm(1024, c0, cs, (2, 3))
                        k.copy("act", Vc[0:cs, :], k.psum[0:cs, 2 * 512:2 * 512 + 1024], r=[k.pbank[2], k.pbank[3]], w=[d_Vc])
                        for h in range(8):
                            k.mm(k.bank(4)[0:cs, h * 32:h * 32 + cs], ke[:, h, c0:c0 + cs], qe[:, h, c0:c0 + cs], True, True,
                                 r=[d_ke, d_qe], w=[k.pbank[4]], sig=(h == 7))
                        k.stt("dve", At[0:cs, :, 0:cs], k.bank(4)[0:cs, 0:256].rearrange("p (h t) -> p h t", h=8)[:, :, 0:cs],
                              1e30, mask[0:cs, 0:cs].unsqueeze(1).to_broadcast([cs, 8, cs]), ALU.min, ALU.mult,
                              r=[k.pbank[4], C["dep"]], w=[d_At])
                        for h in range(8):
                            ob_ = k.bank(5 + h // 4)[0:cs, (h % 4) * 128:(h % 4) * 128 + 128]
                            k.mm(ob_, At[0:cs, h, 0:cs], Vc[0:cs, h * 128:(h + 1) * 128], True, False, r=[d_At, d_Vc],
                                 w=[k.pbank[5 + h // 4]], sig=False)
                            k.mm(ob_, qe[:, h, c0:c0 + cs], Sb[:, h, :], False, True, r=[d_qe, d_Sb],
                                 w=[k.pbank[5 + h // 4]], sig=(h % 4 == 3))
                        for h in range(8):
                            k.mm(k.bank(h // 4)[0:cs, (h % 4) * 128:(h % 4) * 128 + 128], kd[:, h, c0:c0 + cs], C["ident"][:, :],
                                 True, True, r=[d_kd, C["dep"]], w=[k.pbank[h // 4]], sig=(h % 4 == 3))
                        k.copy("dve", kdt[0:cs, :], k.psum[0:cs, 0:1024], r=[k.pbank[0], k.pbank[1]], w=[d_kdt])
                        for h in range(8):
                            k.mm(k.bank(2 + h // 4)[:, (h % 4) * 128:(h % 4) * 128 + 128], kdt[0:cs, h * 128:(h + 1) * 128],
                                 Vc[0:cs, h * 128:(h + 1) * 128], True, True, r=[d_kdt, d_Vc], w=[k.pbank[2 + h // 4]],
                                 sig=(h % 4 == 3))
                        k.tt("dve", S[:, :, :], S[:, :, :], e4[:, :, c, Lidx:Lidx + 1].to_broadcast([128, 8, 128]), ALU.mult,
                             r=[deb], w=[d_S])
                        k.tt("dve", S[:, :, :], S[:, :, :], k.psum[:, 2 * 512:2 * 512 + 1024].rearrange("p (h e) -> p h e", h=8),
                             ALU.add, r=[k.pbank[2], k.pbank[3]], w=[d_S])
                        k.copy("act", Sb[:, :, :], S[:, :, :], r=[d_S], w=[d_Sb])
                        o_ps = k.psum[0:cs, 5 * 512:5 * 512 + 1024]
                        if fwd:
                            o_ = osb[cnt["os"] % 2]
                            do_ = d_osb[cnt["os"] % 2]
                            dso = ds_os[cnt["os"] % 2]
                            cnt["os"] += 1
                            k.copy("act", o_[0:cs, :], o_ps, r=[k.pbank[5], k.pbank[6]], w=[do_])
                            k.dma("sp", OF[si][tc0:tc0 + cs, :], o_[0:cs, :], dso, r=[do_], w=[d_OF[si]])
                            continue
                        k.dma("sp", ofb[0:cs, :], OF[si][tc0:tc0 + cs, :], ds_of, r=[d_OF[si]], w=[d_of])
                        k.tt("dve", ofb[0:cs, :], ofb[0:cs, :], o_ps, ALU.add, r=[k.pbank[5], k.pbank[6]], w=[d_of])
                        k.tt("pool", sqb[0:cs, :], ofb[0:cs, :], ofb[0:cs, :], ALU.mult, r=[d_of], w=[d_sq])
                        k.op("dve", (lambda ss_=ss[0:cs, 0:8], sq_=sqb[0:cs, :].rearrange("p (h e) -> p h e", h=8):
                                     (lambda e: e.tensor_reduce(out=ss_, in_=sq_, axis=mybir.AxisListType.X, op=ALU.add)))(),
                             r=[d_sq], w=[d_ss])
                        k.ts("dve", ss[0:cs, 0:8], ss[0:cs, 0:8], 1.0 / 128.0, float(RMS_EPS), ALU.mult, ALU.add, w=[d_ss])
                        k.tt("pool", ss[0:cs, 0:8], ss[0:cs, 0:8], C["neghalf"][0:cs, 0:1].to_broadcast([cs, 8]), ALU.pow,
                             r=[C["dep2"]], w=[d_ss])
                        o3 = ofb[0:cs, :].rearrange("p (h e) -> p h e", h=8)
                        k.tt("dve", o3, o3, ss[0:cs, 0:8].unsqueeze(2).to_broadcast([cs, 8, 128]), ALU.mult, r=[d_ss], w=[d_of])
                        k.tt("pool", o3, o3, ng[0:cs, :].unsqueeze(1).to_broadcast([cs, 8, 128]), ALU.mult, r=[dlb], w=[d_of])
                        proj_tm(3072, c0, cs, (0, 1))
                        k.act(sgl[0:cs, :], k.psum[0:cs, 0:1024], AF.Silu, r=[k.pbank[0], k.pbank[1]], w=[d_sgl])
                        k.tt("dve", ob[0:cs, :], ofb[0:cs, :], sgl[0:cs, :], ALU.mult, r=[d_of, d_sgl], w=[d_ob])
                        tloc = tc0 - tg0
                        for c8 in range(8):
                            k.mm(k.bank(4)[:, c8 * 32:c8 * 32 + cs], ob[0:cs, c8 * 128:(c8 + 1) * 128], C["ident"][0:cs, 0:cs],
                                 True, True, r=[d_ob, C["dep"]], w=[k.pbank[4]], sig=(c8 == 7))
                        k.copy("act", oT[:, :, tloc:tloc + cs],
                               k.bank(4)[:, 0:256].rearrange("p (c t) -> p c t", c=8)[:, :, 0:cs], r=[k.pbank[4]], w=[d_oT])
                        if tloc % 128 == 0:
                            ti = tloc // 128
                            nv = min(128, ng_ - tloc)
                            hbuf = hbufs[cnt["hi"] % 2]
                            hdep = d_hbuf[cnt["hi"] % 2]
                            dsh = ds_h[cnt["hi"] % 2]
                            cnt["hi"] += 1
                            k.dma("sp", hbuf[0:nv, :], seq.H[src][tc0:tc0 + nv, :], dsh, r=[seq.dH[src][g]], w=[hdep])
                            y_ps = k.psum[:, 5 * 512:5 * 512 + 1024]
                            for half in range(2):
                                for c8 in range(8):
                                    k.mm(k.bank(5 + half)[0:nv, :], oT[:, c8, tloc:tloc + nv], wout[:, c8, half * 512:(half + 1) * 512],
                                         c8 == 0, c8 == 7, r=[wdep, d_oT], w=[k.pbank[5 + half]], sig=(c8 == 7))
                            ln_epilogue(k, C, seq, dst, g, tc0, nv, ti, y_ps, [k.pbank[5], k.pbank[6]], hbuf, hdep, L["gam"],
                                        L["bet"], L["cdep"], (L["stats"], L["mv"], L["rstd"], L["hb"], L["d_s"], L["d_hb"], dsh),
                                        L["hT_out"], L["d_hTo"], 7)
                if not fwd:
                    k.dma("sp", HTo[:, :, tg0:tg0 + ng_], L["hT_out"][:, :, 0:ng_], L["ds_o"], r=[L["d_hTo"]],
                          w=[seq.dHT[dst][g]])
            if not fwd:
                seq.cur = dst


def prologue(k, C, seqs, xs, meta):
    k.sub_begin()
    k.reset_arena(C["keep"])
    xb = [k.f32(D) for _ in range(2)]
    hb = k.bf(D)
    hT_out = k.bf(8 * 512).rearrange("p (c n) -> p c n", c=8)
    dx = [Dep(), Dep()]
    d_hb, d_hTo = Dep(), Dep()
    ds = [k.new_dsem(f"pro{i}") for i in range(2)]
    ds_o = k.new_dsem("pro_o")
    i = 0
    for seq, x in zip(seqs, xs):
        HTo = seq.HT[0].rearrange("(c p) t -> p c t", p=128)
        for g in range(seq.ngroups):
            t0, n = seq.group(g)
            ntile = (n + 127) // 128
            for ti in range(ntile):
                nv = min(128, n - ti * 128)
                tt0 = t0 + ti * 128
                b = xb[i % 2]
                d = dx[i % 2]
                dsx = ds[i % 2]
                i += 1
                if tt0 == 0:
                    k.dma("sp", b[0:N_META, :], meta[:, :], dsx, w=[d])
                    k.dma("sp", b[N_META:nv, :], x[0:nv - N_META, :], dsx, w=[d])
                else:
                    k.dma("sp", b[0:nv, :], x[tt0 - N_META:tt0 - N_META + nv, :], dsx, w=[d])
                k.dma("sp", seq.H[0][tt0:tt0 + nv, :], b[0:nv, :], dsx, r=[d], w=[seq.dH[0][g]])
                k.copy("act", hb[0:nv, :], b[0:nv, :], r=[d], w=[d_hb])
                transpose_rows(k, C, hb, d_hb, nv, hT_out, d_hTo, ti * 128, 7)
            k.dma("sp", HTo[:, :, t0:t0 + n], hT_out[:, :, 0:n], ds_o, r=[d_hTo], w=[seq.dHT[0][g]])
        seq.cur = 0


def epilogue_out(k, seqs, outs):
    ds = k.new_dsem("outs")
    for seq, o in zip(seqs, outs):
        deps = seq.dH[seq.cur]
        n = seq.T - N_META
        step = 1024
        for r0 in range(0, n, step):
            r1 = min(n, r0 + step)
            k.dma("sp", o[r0:r1, :], seq.H[seq.cur][N_META + r0:N_META + r1, :], ds, r=deps)


CONST_COLS = 384


def make_consts():
    c = np.zeros((128, CONST_COLS), np.float32)
    c[:, 0:128] = np.eye(128, dtype=np.float32)
    p = np.arange(128)[:, None]
    f = np.arange(128)[None, :]
    c[:, 128:256] = (f <= p)
    c[:, 256:384] = (p <= f)
    return c


WNAMES = ["meta_tokens", "hg_lb_logits", "a_w_in", "a_w_out", "a_norm_g", "b_w_grp", "b_scale", "c_w_qkv", "c_w_out",
          "c_sink", "d_w_dq", "d_q_norm_g", "d_w_uq", "d_w_dkv", "d_kv_norm_g", "d_w_ukv", "d_w_out", "ffn_w_gu",
          "ffn_w_down", "ln_g", "ln_b"]
WSHAPES = {
    "meta_tokens": (16, 1024), "hg_lb_logits": (5, 1024), "a_w_in": (1, 1024, 5120), "a_w_out": (1, 1024, 1024),
    "a_norm_g": (1, 128), "b_w_grp": (1, 4, 256, 256), "b_scale": (1, 1024), "c_w_qkv": (1, 1024, 1536),
    "c_w_out": (1, 1024, 1024), "c_sink": (1, 8), "d_w_dq": (1, 1024, 256), "d_q_norm_g": (1, 256),
    "d_w_uq": (1, 256, 3072), "d_w_dkv": (1, 1024, 320), "d_kv_norm_g": (1, 256), "d_w_ukv": (1, 256, 4096),
    "d_w_out": (1, 2048, 1024), "ffn_w_gu": (4, 1024, 5632), "ffn_w_down": (4, 2816, 1024), "ln_g": (4, 2, 1024),
    "ln_b": (4, 2, 1024),
}


def build_program(Ts, plan):
    nc = bass.Bass("TRN2", target_bir_lowering=False)
    xs = [nc.dram_tensor(f"x{i}", [T - N_META, D], F32, kind="ExternalInput").ap() for i, T in enumerate(Ts)]
    outs = [nc.dram_tensor(f"y{i}", [T - N_META, D], F32, kind="ExternalOutput").ap() for i, T in enumerate(Ts)]
    W = {n: nc.dram_tensor(n, list(WSHAPES[n]), F32, kind="ExternalInput").ap() for n in WNAMES}
    consts = nc.dram_tensor("consts", [128, CONST_COLS], F32, kind="ExternalInput").ap()
    k = KB(nc)
    seqs = [Seq(nc, f"s{i}", T) for i, T in enumerate(Ts)]
    invcnt = [nc.dram_tensor(f"invcnt{i}", [4, T], F32, kind="ExternalInput").ap() for i, T in enumerate(Ts)]
    rope128 = [nc.dram_tensor(f"rope128_{i}", [2, 128, T], F32, kind="ExternalInput").ap() for i, T in enumerate(Ts)]
    rope64 = [nc.dram_tensor(f"rope64_{i}", [2, 64, T], F32, kind="ExternalInput").ap() for i, T in enumerate(Ts)]
    C = {}
    cc = k.bf(384)
    C["ident"] = cc[:, 0:128]
    C["mprev"] = cc[:, 128:256]
    C["mnext"] = cc[:, 256:384]
    C["dep"] = Dep()
    dsc = k.new_dsem("consts")
    k.dma("pool", cc, consts[:, 0:384], dsc, w=[C["dep"]])
    C["ones"] = k.bf(128)
    C["neghalf"] = k.f32(8)
    C["dep2"] = Dep()
    k.memset("pool", C["neghalf"], -0.5, w=[C["dep2"]])
    k.memset("pool", C["ones"], 1.0, w=[C["dep2"]])
    C["invcnt"] = invcnt
    C["rope128"] = rope128
    C["rope64"] = rope64
    C["keep"] = k.aoff
    k.dsem_base = k.dsem_idx
    prologue(k, C, seqs, xs, W["meta_tokens"])
    for name in plan:
        kind, li = name.split(":")
        li = int(li)
        if kind == "ffn":
            ffn_sublayer(k, C, seqs, W, li)
        elif kind == "mix" and li % 4 == 0:
            hgrn2_sublayer(k, C, seqs, W, li)
        elif kind == "mix" and li % 4 == 1:
            pool_sublayer(k, C, seqs, W, li)
        elif kind == "mix" and li % 4 == 2:
            swa_sublayer(k, C, seqs, W, li)
        elif kind == "mix" and li % 4 == 3:
            mla_sublayer(k, C, seqs, W, li)
        else:
            raise ValueError(name)
    k.barrier()
    epilogue_out(k, seqs, outs)
    k.barrier()
    k.emit()
    return nc, k


FULL_PLAN = ["mix:0", "ffn:0", "mix:1", "ffn:1", "mix:2", "ffn:2", "mix:3", "ffn:3"]


def run(inputs, Ts, plan, assign, n_cores=8, trace=False):
    nc, k = build_program(Ts, plan)
    consts = make_consts()
    in_maps = []
    for c in range(n_cores):
        m = {n: np.ascontiguousarray(inputs[n], dtype=np.float32) for n in WNAMES}
        m["consts"] = consts
        for i, T in enumerate(Ts):
            m[f"invcnt{i}"] = make_invcnt(T)
            m[f"rope128_{i}"] = make_rope_fm(T, 128)
            m[f"rope64_{i}"] = make_rope_fm(T, 64)
        for i, (nm, b) in enumerate(assign[c]):
            m[f"x{i}"] = np.ascontiguousarray(inputs[nm][b][:Ts[i] - N_META], dtype=np.float32)
        in_maps.append(m)
    res = run_bass_kernel_spmd(nc, in_maps, core_ids=list(range(n_cores)), trace=trace)
    return res, k


def kernel(**inputs):
    Ts = [4096 + N_META, 8192 + N_META]
    assign = [[("x_prompt", c), ("x_sample", c // 4)] for c in range(8)]
    res, _ = run(inputs, Ts, FULL_PLAN, assign)
    y_prompt = np.stack([res.results[c]["y0"] for c in range(8)], axis=0)
    y_sample = np.stack([res.results[0]["y1"], res.results[4]["y1"]], axis=0)
    return (y_prompt.astype(np.float32), y_sample.astype(np.float32))
```
